# Optimizing a Trainium2 kernel written in Bass

```python
import jax, jax.numpy as jnp
from jax import lax
import numpy as np

D_MODEL = 1024
BATCH = 8
SEQ = 2048
DEPTH = 2
DEC_BATCH = 128
DEC_SEQ = 8
PAST_LEN = 2048
PAGE_SIZE = 128

N_HEADS = 8
HEAD_DIM = D_MODEL // 16
ATTN_WIDTH = N_HEADS * HEAD_DIM
FORGET_BIAS_INIT = 3.0
Q_BLOCK = 128
SC_WIDTH = D_MODEL // 4
SC_KERNEL = 3
CF_WIDTH = D_MODEL // 4
CF_KERNEL = 31
FFN_HIDDEN = -(-8 * D_MODEL // (3 * 256)) * 256
N_BRANCHES = 3
EPS = 1e-6

OFF_Q = 0
OFF_K = OFF_Q + ATTN_WIDTH
OFF_V = OFF_K + ATTN_WIDTH
OFF_F = OFF_V + ATTN_WIDTH
OFF_SB = OFF_F + N_HEADS
OFF_SC = OFF_SB + SC_WIDTH
OFF_SH = OFF_SC + SC_WIDTH
OFF_GLU = OFF_SH + SC_WIDTH
OFF_GATE = OFF_GLU + 2 * CF_WIDTH
P_TOTAL = OFF_GATE + N_BRANCHES * D_MODEL

kernel_name = "hybrid_conv_conformer_fox_decode_step"


def rms_norm(x, g):
    x32 = x.astype(jnp.float32)
    y = x32 * lax.rsqrt(jnp.mean(x32 * x32, axis=-1, keepdims=True) + EPS)
    return (y * g.astype(jnp.float32)).astype(x.dtype)


def layer_norm(x, g, b):
    x32 = x.astype(jnp.float32)
    mu = jnp.mean(x32, axis=-1, keepdims=True)
    xc = x32 - mu
    var = jnp.mean(xc * xc, axis=-1, keepdims=True)
    y = xc * lax.rsqrt(var + EPS) * g.astype(jnp.float32) + b.astype(jnp.float32)
    return y.astype(x.dtype)


def causal_depthwise_conv(u, buf, w):
    width = w.shape[0]
    full = jnp.concatenate([buf.astype(u.dtype), u], axis=1)
    out = lax.conv_general_dilated(
        full, w[:, None, :].astype(u.dtype), window_strides=(1,), padding='VALID',
        dimension_numbers=('NWC', 'WIO', 'NWC'), feature_group_count=u.shape[-1])
    new_buf = full[:, full.shape[1] - (width - 1):]
    return out, new_buf


def forgetting_attention(q, k, v, logf):
    B, Tq, H, Dh = q.shape
    Tk = k.shape[1]
    c = jnp.cumsum(logf.astype(jnp.float32), axis=1)
    ckT = c.transpose(0, 2, 1)
    cq = c[:, Tk - Tq:]
    qb = Q_BLOCK if Tq % Q_BLOCK == 0 else Tq
    nb = Tq // qb
    q_blocks = (q.astype(jnp.float32) * (Dh ** -0.5)).reshape(B, nb, qb, H, Dh).transpose(1, 0, 2, 3, 4)
    cq_blocks = cq.reshape(B, nb, qb, H).transpose(1, 0, 3, 2)
    qpos = (Tk - Tq + jnp.arange(Tq)).reshape(nb, qb)
    kpos = jnp.arange(Tk)
    k32 = k.astype(jnp.float32)
    v32 = v.astype(jnp.float32)

    def block(args):
        qblk, cqblk, qp = args
        s = jnp.einsum('bqhd,bkhd->bhqk', qblk, k32)
        s = s + cqblk[..., None] - ckT[:, :, None, :]
        s = jnp.where(kpos[None, None, None, :] <= qp[None, None, :, None], s, -jnp.inf)
        p = jax.nn.softmax(s, axis=-1)
        return jnp.einsum('bhqk,bkhd->bqhd', p, v32)

    o = lax.map(block, (q_blocks, cq_blocks, qpos))
    return o.transpose(1, 0, 2, 3, 4).reshape(B, Tq, H * Dh).astype(q.dtype)


def hybrid_layer(x, past_k, past_v, past_logf, buf_a, buf_b, w):
    B, T, _ = x.shape
    h = rms_norm(x, w['norm1_g'])
    p = h @ w['w_in']
    q = p[..., OFF_Q:OFF_Q + ATTN_WIDTH].reshape(B, T, N_HEADS, HEAD_DIM)
    k = p[..., OFF_K:OFF_K + ATTN_WIDTH].reshape(B, T, N_HEADS, HEAD_DIM)
    v = p[..., OFF_V:OFF_V + ATTN_WIDTH].reshape(B, T, N_HEADS, HEAD_DIM)
    logf = jax.nn.log_sigmoid(p[..., OFF_F:OFF_F + N_HEADS].astype(jnp.float32)
                              + w['b_f'].astype(jnp.float32))
    k_all = jnp.concatenate([past_k.astype(k.dtype), k], axis=1)
    v_all = jnp.concatenate([past_v.astype(v.dtype), v], axis=1)
    logf_all = jnp.concatenate([past_logf.astype(jnp.float32), logf], axis=1)
    y_c = forgetting_attention(q, k_all, v_all, logf_all) @ w['w_c_out']
    sb = p[..., OFF_SB:OFF_SB + SC_WIDTH]
    sc = p[..., OFF_SC:OFF_SC + SC_WIDTH]
    sh = p[..., OFF_SH:OFF_SH + SC_WIDTH]
    z_a, new_buf_a = causal_depthwise_conv(sc * sh, buf_a, w['conv_a_w'])
    y_a = (sb * z_a) @ w['w_a_out']
    glu_a = p[..., OFF_GLU:OFF_GLU + CF_WIDTH]
    glu_b = p[..., OFF_GLU + CF_WIDTH:OFF_GLU + 2 * CF_WIDTH]
    z_b, new_buf_b = causal_depthwise_conv(glu_a * jax.nn.sigmoid(glu_b), buf_b, w['conv_b_w'])
    z_b = layer_norm(z_b + w['conv_b_bias'].astype(z_b.dtype), w['cf_norm_g'], w['cf_norm_b'])
    y_b = jax.nn.silu(z_b) @ w['w_b_out']
    gates = jax.nn.sigmoid(p[..., OFF_GATE:].reshape(B, T, N_BRANCHES, D_MODEL))
    merged = gates[:, :, 0] * y_a + gates[:, :, 1] * y_b + gates[:, :, 2] * y_c
    x = x + merged @ w['w_o']
    h2 = rms_norm(x, w['norm2_g'])
    x = x + (jax.nn.silu(h2 @ w['w_ffn_gate']) * (h2 @ w['w_ffn_up'])) @ w['w_ffn_down']
    return x, k, v, logf, new_buf_a, new_buf_b


def setup_inputs(seed: int = 0) -> dict:
    key = jax.random.key(seed)
    ks = jax.random.split(key, 32)
    f32 = jnp.float32
    n_pages = PAST_LEN // PAGE_SIZE
    n_used = DEC_BATCH * n_pages
    n_pool = n_used + max(1, n_used // 4)

    def nrm(k, shape, scale):
        return jax.random.normal(k, shape, f32) * scale

    return {
        'x_prompt': nrm(ks[0], (BATCH, SEQ, D_MODEL), 1.0),
        'x_sample': nrm(ks[1], (DEC_BATCH, DEC_SEQ, D_MODEL), 1.0),
        'cache_k': nrm(ks[2], (DEPTH, n_pool, PAGE_SIZE, N_HEADS, HEAD_DIM), 1.0),
        'cache_v': nrm(ks[3], (DEPTH, n_pool, PAGE_SIZE, N_HEADS, HEAD_DIM), 1.0),
        'cache_logf': jax.nn.log_sigmoid(FORGET_BIAS_INIT + nrm(ks[4], (DEPTH, n_pool, PAGE_SIZE, N_HEADS), 1.0)),
        'state_conv_a': nrm(ks[5], (DEPTH, DEC_BATCH, SC_KERNEL - 1, SC_WIDTH), 1.0),
        'state_conv_b': nrm(ks[6], (DEPTH, DEC_BATCH, CF_KERNEL - 1, CF_WIDTH), 1.0),
        'page_table': jax.random.permutation(ks[7], n_pool)[:n_used].reshape(DEC_BATCH, n_pages).astype(jnp.int32),
        'norm1_g': 1.0 + nrm(ks[8], (DEPTH, D_MODEL), 0.02),
        'w_in': nrm(ks[9], (DEPTH, D_MODEL, P_TOTAL), D_MODEL ** -0.5),
        'b_f': FORGET_BIAS_INIT + nrm(ks[10], (DEPTH, N_HEADS), 0.1),
        'conv_a_w': nrm(ks[11], (DEPTH, SC_KERNEL, SC_WIDTH), SC_KERNEL ** -0.5),
        'conv_b_w': nrm(ks[12], (DEPTH, CF_KERNEL, CF_WIDTH), CF_KERNEL ** -0.5),
        'conv_b_bias': nrm(ks[13], (DEPTH, CF_WIDTH), 0.02),
        'cf_norm_g': 1.0 + nrm(ks[14], (DEPTH, CF_WIDTH), 0.02),
        'cf_norm_b': nrm(ks[15], (DEPTH, CF_WIDTH), 0.02),
        'w_a_out': nrm(ks[16], (DEPTH, SC_WIDTH, D_MODEL), SC_WIDTH ** -0.5),
        'w_b_out': nrm(ks[17], (DEPTH, CF_WIDTH, D_MODEL), CF_WIDTH ** -0.5),
        'w_c_out': nrm(ks[18], (DEPTH, ATTN_WIDTH, D_MODEL), ATTN_WIDTH ** -0.5),
        'w_o': nrm(ks[19], (DEPTH, D_MODEL, D_MODEL), D_MODEL ** -0.5),
        'norm2_g': 1.0 + nrm(ks[20], (DEPTH, D_MODEL), 0.02),
        'w_ffn_gate': nrm(ks[21], (DEPTH, D_MODEL, FFN_HIDDEN), D_MODEL ** -0.5),
        'w_ffn_up': nrm(ks[22], (DEPTH, D_MODEL, FFN_HIDDEN), D_MODEL ** -0.5),
        'w_ffn_down': nrm(ks[23], (DEPTH, FFN_HIDDEN, D_MODEL), FFN_HIDDEN ** -0.5),
        'final_norm_g': 1.0 + nrm(ks[24], (D_MODEL,), 0.02),
    }


def reference(x_prompt, x_sample, cache_k, cache_v, cache_logf, state_conv_a, state_conv_b, page_table,
              norm1_g, w_in, b_f, conv_a_w, conv_b_w, conv_b_bias, cf_norm_g, cf_norm_b,
              w_a_out, w_b_out, w_c_out, w_o, norm2_g, w_ffn_gate, w_ffn_up, w_ffn_down, final_norm_g):
    dec_b, n_pages = page_table.shape
    past_len = n_pages * PAGE_SIZE
    pb, ps, _ = x_prompt.shape
    dt = x_prompt.dtype
    xp, xs = x_prompt, x_sample
    kp_l, vp_l, fp_l, ap_l, bp_l = [], [], [], [], []
    ks_l, vs_l, fs_l, as_l, bs_l = [], [], [], [], []
    for l in range(DEPTH):
        w = {'norm1_g': norm1_g[l], 'w_in': w_in[l], 'b_f': b_f[l], 'conv_a_w': conv_a_w[l],
             'conv_b_w': conv_b_w[l], 'conv_b_bias': conv_b_bias[l], 'cf_norm_g': cf_norm_g[l],
             'cf_norm_b': cf_norm_b[l], 'w_a_out': w_a_out[l], 'w_b_out': w_b_out[l], 'w_c_out': w_c_out[l],
             'w_o': w_o[l], 'norm2_g': norm2_g[l], 'w_ffn_gate': w_ffn_gate[l], 'w_ffn_up': w_ffn_up[l],
             'w_ffn_down': w_ffn_down[l]}
        xp, k, v, f, ba, bb = hybrid_layer(
            xp, jnp.zeros((pb, 0, N_HEADS, HEAD_DIM), dt), jnp.zeros((pb, 0, N_HEADS, HEAD_DIM), dt),
            jnp.zeros((pb, 0, N_HEADS), jnp.float32), jnp.zeros((pb, SC_KERNEL - 1, SC_WIDTH), dt),
            jnp.zeros((pb, CF_KERNEL - 1, CF_WIDTH), dt), w)
        kp_l.append(k); vp_l.append(v); fp_l.append(f); ap_l.append(ba); bp_l.append(bb)
        pk = cache_k[l][page_table].reshape(dec_b, past_len, N_HEADS, HEAD_DIM)
        pv = cache_v[l][page_table].reshape(dec_b, past_len, N_HEADS, HEAD_DIM)
        pf = cache_logf[l][page_table].reshape(dec_b, past_len, N_HEADS)
        xs, k, v, f, ba, bb = hybrid_layer(xs, pk, pv, pf, state_conv_a[l], state_conv_b[l], w)
        ks_l.append(k); vs_l.append(v); fs_l.append(f); as_l.append(ba); bs_l.append(bb)
    y_prompt = rms_norm(xp, final_norm_g)
    y_sample = rms_norm(xs, final_norm_g)
    n_pp = ps // PAGE_SIZE
    k_prompt = jnp.stack(kp_l).reshape(DEPTH, pb, n_pp, PAGE_SIZE, N_HEADS, HEAD_DIM)
    v_prompt = jnp.stack(vp_l).reshape(DEPTH, pb, n_pp, PAGE_SIZE, N_HEADS, HEAD_DIM)
    logf_prompt = jnp.stack(fp_l).reshape(DEPTH, pb, n_pp, PAGE_SIZE, N_HEADS)
    conv_a_prompt = jnp.stack(ap_l)
    conv_b_prompt = jnp.stack(bp_l)
    k_sample = jnp.stack(ks_l)
    v_sample = jnp.stack(vs_l)
    logf_sample = jnp.stack(fs_l)
    conv_a_sample = jnp.stack(as_l)
    conv_b_sample = jnp.stack(bs_l)
    return (y_prompt, y_sample, k_prompt, v_prompt, logf_prompt, conv_a_prompt, conv_b_prompt,
            k_sample, v_sample, logf_sample, conv_a_sample, conv_b_sample)
```

```python
import os
import numpy as np
from contextlib import ExitStack
import concourse.bass as bass
import concourse.mybir as mybir
from concourse.bass_utils import run_bass_kernel_spmd

F32 = mybir.dt.float32
BF16 = mybir.dt.bfloat16
I32 = mybir.dt.int32
AF = mybir.ActivationFunctionType
ALU = mybir.AluOpType
ENGS = ["tensor", "vector", "scalar", "gpsimd", "sync"]

NCORES = 8
D = 1024; KC = 8; NH = 8; DH = 64; PT = 5896; DEPTH = 2
OFF_Q = 0; OFF_K = 512; OFF_V = 1024; OFF_F = 1536; OFF_SB = 1544; OFF_SC = 1800; OFF_SH = 2056
OFF_GA = 2312; OFF_GB = 2568; OFF_GATE = 2824
FFN = 2816; HC = 22
SEQ = 2048; NSEQ = 16; DSEQ = 8; NPAGES = 16; NPOOL = 2560
EPS = 1e-6
PASSES = [(0, 512, False), (512, 512, False), (1024, 512, False), (1536, 512, False), (0, 128, True)]
SP_G1 = 0; SP_G2 = 16; SP_GF = 32; SP_CAW = 40; SP_CBW = 52; SP_CBB = 176; SP_CNG = 180; SP_CNB = 184; SP_BF = 188; SP_N = 204
C_ID = 0; C_ONE = 128; C_L = 256; C_S127 = 384; C_S7 = 512; C_IOTA = 640; C_MASKN = 641; C_SUF = 769; C_BM = 897; C_N = 913


class Buf:
    __slots__ = ("name", "lw", "rd", "dsem", "dcount", "psum")

    def __init__(self, name, psum=False):
        self.name = name; self.lw = None; self.rd = []; self.dsem = None; self.dcount = 0; self.psum = psum


class Prog:
    def __init__(self, nc, stack):
        self.nc = nc; self.stack = stack
        self.ops = {e: [] for e in ENGS}
        self.esem = {}; self.ecount = {e: 0 for e in ENGS}
        self.waited = {e: {} for e in ENGS}; self.sems = {}
        for e in ENGS:
            s = stack.enter_context(nc.semaphore("es_" + e))
            self.esem[e] = s; self.sems["E" + e] = s
        self.nsem = 0; self.out_events = []

    def new_dsem(self, buf):
        if buf.dsem is None:
            self.nsem += 1
            key = "D%d" % self.nsem
            self.sems[key] = self.stack.enter_context(self.nc.semaphore("ds%d" % self.nsem))
            buf.dsem = key
        return buf.dsem

    def _deps(self, eng, reads, writes):
        ev = {}

        def add(e):
            if e is None:
                return
            k, v, en = e
            if en == "tensor" and eng == "tensor":
                return
            if ev.get(k, 0) < v:
                ev[k] = v
        for b in reads:
            add(b.lw)
            if b.psum:
                for r in b.rd:
                    if r[2] != eng:
                        add(r)
        for b in writes:
            add(b.lw)
            for r in b.rd:
                add(r)
        w = self.waited[eng]; out = []
        for k, v in ev.items():
            if w.get(k, 0) < v:
                w[k] = v; out.append((self.sems[k], v))
        return out

    def op(self, eng, fn, reads=(), writes=()):
        waits = self._deps(eng, reads, writes)
        self.ecount[eng] += 1
        n = self.ecount[eng]; sem = self.esem[eng]

        def run(e, waits=waits, fn=fn, sem=sem):
            for s, v in waits:
                e.wait_ge(s, v)
            getattr(e, fn[0])(*fn[1], **fn[2]).then_inc(sem, 1)
        self.ops[eng].append(run)
        evt = ("E" + eng, n, eng)
        for b in reads:
            b.rd.append(evt)
            if len(b.rd) > 64:
                b.rd = _compact(b.rd)
        for b in writes:
            b.lw = evt; b.rd = []
        return evt

    def dma(self, q, fn, sbuf, load, reads=(), writes=(), is_output=False):
        if load:
            waits = self._deps(q, list(reads), [sbuf] + list(writes))
        else:
            waits = self._deps(q, [sbuf] + list(reads), list(writes))
        key = self.new_dsem(sbuf)
        sbuf.dcount += 16
        val = sbuf.dcount; sem = self.sems[key]

        def run(e, waits=waits, fn=fn, sem=sem):
            for s, v in waits:
                e.wait_ge(s, v)
            getattr(e, fn[0])(*fn[1], **fn[2]).then_inc(sem, 16)
        self.ops[q].append(run)
        evt = (key, val, "dma")
        rds = list(reads) + ([] if load else [sbuf])
        wrs = list(writes) + ([sbuf] if load else [])
        for b in rds:
            b.rd.append(evt)
        for b in wrs:
            b.lw = evt; b.rd = []
        if is_output:
            self.out_events.append(evt)
        return evt

    def finish(self):
        fin = {}
        for k, v, _ in self.out_events:
            fin[k] = max(fin.get(k, 0), v)
        finw = [(self.sems[k], v) for k, v in fin.items()]

        def fin_run(e):
            for s, v in finw:
                e.wait_ge(s, v)
        self.ops["sync"].append(fin_run)
        ops = self.ops
        with self.nc.Block() as block:
            @block.tensor
            def _(e):
                for f in ops["tensor"]:
                    f(e)

            @block.vector
            def _(e):
                for f in ops["vector"]:
                    f(e)

            @block.scalar
            def _(e):
                for f in ops["scalar"]:
                    f(e)

            @block.gpsimd
            def _(e):
                for f in ops["gpsimd"]:
                    f(e)

            @block.sync
            def _(e):
                for f in ops["sync"]:
                    f(e)


def _I(name, *a, **kw):
    return (name, a, kw)


def _compact(evs):
    m = {}
    for k, v, en in evs:
        if k not in m or m[k][1] < v:
            m[k] = (k, v, en)
    return list(m.values())


KDBG = int(os.environ.get("KDBG", "99"))


class _Stop(Exception):
    pass


def build_nc(passes=None, npool=NPOOL, stop=None):
    passes = passes or PASSES
    nc = bass.Bass("TRN2", target_bir_lowering=False)
    di = lambda n, s, dt=F32: nc.dram_tensor(n, s, dt, kind="ExternalInput").ap()
    do = lambda n, s, dt=F32: nc.dram_tensor(n, s, dt, kind="ExternalOutput").ap()
    xp = di("xp", [SEQ, D]); xs = di("xs", [128, D])
    cache_k = di("cache_k", [DEPTH * npool * 128, 512]); cache_v = di("cache_v", [DEPTH * npool * 128, 512])
    cache_f = di("cache_f", [DEPTH * npool * 128, 8])
    sca = di("sca", [DEPTH, NSEQ * 2, 256]); scb = di("scb", [DEPTH, NSEQ * 30, 256])
    ptb = di("ptb", [1, NSEQ * NPAGES], I32)
    smallp = di("smallp", [128, SP_N]); consts = di("consts", [128, C_N])
    w_in = di("w_in", [DEPTH, D, PT]); w_a = di("w_a", [DEPTH, 256, D]); w_b = di("w_b", [DEPTH, 256, D])
    w_c = di("w_c", [DEPTH, 512, D]); w_o = di("w_o", [DEPTH, D, D])
    w_g = di("w_g", [DEPTH, D, FFN]); w_u = di("w_u", [DEPTH, D, FFN]); w_d = di("w_d", [DEPTH, FFN, D])
    y_p = do("y_p", [SEQ, D]); y_s = do("y_s", [128, D])
    k_p = do("k_p", [DEPTH, SEQ, 512]); v_p = do("v_p", [DEPTH, SEQ, 512]); lf_p = do("lf_p", [DEPTH, SEQ, 8])
    ca_p = do("ca_p", [DEPTH, 2, 256]); cb_p = do("cb_p", [DEPTH, 30, 256])
    k_s = do("k_s", [DEPTH, 128, 512]); v_s = do("v_s", [DEPTH, 128, 512]); lf_s = do("lf_s", [DEPTH, 128, 8])
    ca_s = do("ca_s", [DEPTH, NSEQ * 2, 256]); cb_s = do("cb_s", [DEPTH, NSEQ, 30, 256])
    kt_scr = nc.dram_tensor("kt_scr", [DEPTH, 128, 4, SEQ], BF16).ap()
    vp_scr = nc.dram_tensor("vp_scr", [DEPTH, 128, 16, 520], BF16).ap()
    c_scr = nc.dram_tensor("c_scr", [DEPTH, 128, 16, 8], F32).ap()

    with ExitStack() as st:
        P = Prog(nc, st)
        sb = lambda name, shape, dt: st.enter_context(nc.sbuf_tensor(name, shape, dt))
        V = lambda fn, r=(), w=(): P.op("vector", fn, r, w)
        A = lambda fn, r=(), w=(): P.op("scalar", fn, r, w)
        T = lambda fn, r=(), w=(): P.op("tensor", fn, r, w)

        banks = [st.enter_context(nc.psum_tensor("pb%d" % i, [128, 512], F32)) for i in range(6)]
        bankB = [Buf("pb%d" % i, True) for i in range(6)]
        free = list(range(6))
        tbanks = [st.enter_context(nc.psum_tensor("tb%d" % i, [128, 1024], BF16)) for i in range(2)]
        tbankB = [Buf("tb%d" % i, True) for i in range(2)]
        tbi = [0]

        def tget():
            i = tbi[0] % 2; tbi[0] += 1
            return tbanks[i], tbankB[i]

        def ps_get():
            i = free.pop(0)
            return i

        def ps_rel(i):
            free.append(i)

        cst = sb("cst", [128, C_N], F32); cstB = Buf("cst")
        cstb = sb("cstb", [128, 512], BF16); cstbB = Buf("cstb")
        smp = sb("smp", [128, SP_N], F32); smpB = Buf("smp")
        xT = sb("xT", [128, 8, 512], F32); xTB = [Buf("xT%d" % i) for i in range(8)]
        hT = sb("hT", [128, 8, 512], BF16); hTB = Buf("hT")
        QT = sb("QT", [128, 4, 512], BF16); QTB = Buf("QT")
        KVR = sb("KVR", [128, 8192 + 8320], BF16)
        KT = KVR[:, 0:8192].rearrange("p (j n) -> p j n", j=4)
        Vp = KVR[:, 8192:8192 + 8320].rearrange("p (b h e) -> p b h e", b=16, h=8)
        KTB = [Buf("KT%d" % i) for i in range(16)]; VpB = [Buf("Vp%d" % i) for i in range(16)]
        ctok = sb("ctok", [128, 16, 8], F32); ctokB = Buf("ctok")
        totA = sb("tot", [128, DEPTH, 8], F32); totBs = [Buf("tot%d" % i) for i in range(DEPTH)]
        OT = sb("OT", [65, 8, 512], BF16); OTB = Buf("OT")
        ua = sb("ua", [128, 2, 2 + 512], F32); uaB = Buf("ua")
        ub = sb("ub", [128, 2, 30 + 512], F32); ubB = Buf("ub")
        uaS = ua[:, :, 0:160].rearrange("p c (b t) -> p c b t", t=10)
        ubS = sb("ubS", [128, 2, 16, 38], F32)
        sbT = sb("sbT", [128, 2, 512], F32); sbTB = Buf("sbT")
        tmpA = sb("tmpA", [128, 2, 512], F32); tmpAB = Buf("tmpA")
        zb = sb("zb", [128, 2, 512], F32); zbB = Buf("zb")
        AinT = sb("AinT", [128, 2, 512], BF16); AinB = Buf("Ain")
        BinT = sb("BinT", [128, 2, 512], BF16); BinB = Buf("Bin")
        histA = sb("histA", [128, DEPTH, 2, 2], F32); histB_ = sb("histB", [128, DEPTH, 2, 30], F32)
        histAB = [Buf("hA%d" % l) for l in range(DEPTH)]; histBB = [Buf("hB%d" % l) for l in range(DEPTH)]
        f1 = [sb("f1_%d" % i, [128, 512], F32) for i in range(4)]; f1B = [Buf("f1_%d" % i) for i in range(4)]
        f1i = [0]
        pTt = [sb("pT%d" % i, [128, 512], BF16) for i in range(3)]; pTB = [Buf("pT%d" % i) for i in range(3)]
        stg = [sb("stg%d" % i, [128, 512], F32) for i in range(4)]; stgB = [Buf("stg%d" % i) for i in range(4)]
        stgi = [0]
        kbf = sb("kbf", [128, 512], BF16); kbfB = Buf("kbf")
        lft = sb("lft", [128, 8], F32); lftB = Buf("lft")
        sm8 = [sb("sm8_%d" % i, [128, 8], F32) for i in range(3)]; sm8B = [Buf("sm8_%d" % i) for i in range(3)]
        biasT = sb("biasT", [128, 16, 8], F32); biasTB = Buf("biasT")
        rl = sb("rl", [65, 512], F32); rlB = Buf("rl")
        rstdT = sb("rstdT", [128, 512], F32); rstdB = Buf("rstd")
        mrgT = sb("mrgT", [128, 8, 512], BF16); mrgB = Buf("mrg")
        aT = KVR[:, 0:HC * 512].rearrange("p (k n) -> p k n", k=HC); aTB = [Buf("aT%d" % i) for i in range(HC)]
        NSLOT = 3
        wsl = [sb("wsl%d" % i, [128, 4096], BF16) for i in range(NSLOT)]; wslB = [Buf("wsl%d" % i) for i in range(NSLOT)]
        wi = [0]
        idxf = sb("idxf", [128, 256], F32); idxi = sb("idxi", [128, 512], mybir.dt.uint32); idxB = Buf("idx")
        Kt = [KVR[:, i * 2048:(i + 1) * 2048].rearrange("p (a n) -> p a n", a=4) for i in range(2)]; KtB = [Buf("Kt%d" % i) for i in range(2)]
        Vt = [KVR[:, 8192 + i * 2048:8192 + (i + 1) * 2048].rearrange("p (a n) -> p a n", a=4) for i in range(2)]; VtB = [Buf("Vt%d" % i) for i in range(2)]
        KTs = [KVR[:, 4096 + i * 2048:4096 + (i + 1) * 2048].rearrange("p (a n) -> p a n", a=4) for i in range(2)]; KTsB = [Buf("KTs%d" % i) for i in range(2)]
        lfS = sb("lfS", [128, 16, 8], F32); lfSB = Buf("lfS")
        lfn = sb("lfn", [128, 8], F32); lfnB = Buf("lfn")
        lfnew = sb("lfnew", [128, 8], F32); lfnewB = Buf("lfnew")
        NTs = sb("NTs", [128, 136], F32); NTsB = Buf("NTs")
        lfblk = sb("lfblk", [128, 128], F32); lfblkB = Buf("lfblk")
        pTn = sb("pTn", [128, 64], BF16); pTnB = Buf("pTn")
        Qbd = sb("Qbd", [128, 4, 16, 16], BF16); QbdB = Buf("Qbd")
        totS = sb("totS", [128, 16, 8], F32); totSB = Buf("totS")
        bpast = sb("bpast", [128, 16, 8], F32); bpastB = Buf("bpast")
        bnew = sb("bnew", [128, 8], F32); bnewB = Buf("bnew")
        spS = sb("spS", [128, 256], F32); spSB = Buf("spS")
        pTs = sb("pTs", [128, 256], BF16); pTsB = Buf("pTs")
        pTsX = sb("pTsX", [128, 256], BF16); pTs2 = [pTs, pTsX]; pTs2B = [pTsB, Buf("pTsX")]
        spN = sb("spN", [128, 64], F32); spNB = Buf("spN")
        Oacc = sb("Oacc", [64, 128], F32); OaccB = Buf("Oacc")
        KTn = KVR[:, 12352:12864].rearrange("p (a n) -> p a n", a=4); KTnB = Buf("KTn")
        Vnew = KVR[:, 12864:13376]; VnewB = Buf("Vnew")
        Vn = KVR[:, 13376:13888]; VnB = Buf("Vn")
        stS = sb("stS", [128, 256], F32); stSB = Buf("stS")

        identf = cst[:, C_ID:C_ID + 128]; onesf = cst[:, C_ONE:C_ONE + 128]; Lf = cst[:, C_L:C_L + 128]
        s127 = cst[:, C_S127:C_S127 + 128]; s7 = cst[:, C_S7:C_S7 + 128]
        identb = cstb[:, 0:128]; onesb = cstb[:, 128:256]; trib = cstb[:, 256:384]; maskNb = cstb[:, 384:512]
        sufmat = cst[:, C_SUF:C_SUF + 128]; bmask = cst[:, C_BM:C_BM + 16]

        P.dma("sync", _I("dma_start", out=cst[:], in_=consts), cstB, True)
        P.dma("sync", _I("dma_start", out=smp[:], in_=smallp), smpB, True)
        V(_I("tensor_copy", out=cstb[:, 0:384], in_=cst[:, 0:384]), [cstB], [cstbB])
        V(_I("tensor_copy", out=cstb[:, 384:512], in_=cst[:, C_MASKN:C_MASKN + 128]), [cstB], [cstbB])
        P.dma("sync", _I("dma_start", out=idxi[:, 0:256].bitcast(I32), in_=ptb.to_broadcast([128, 256])), idxB, True)
        V(_I("tensor_copy", out=idxf[:], in_=idxi[:, 0:256].bitcast(I32)), [idxB], [idxB])
        V(_I("tensor_scalar", out=idxf[:], in0=idxf[:], scalar1=128.0, scalar2=cst[:, C_IOTA:C_IOTA + 1],
                                    op0=ALU.mult, op1=ALU.add), [idxB, cstB], [idxB])
        V(_I("tensor_copy", out=idxi[:, 0:256], in_=idxf[:]), [idxB], [idxB])
        V(_I("tensor_scalar", out=idxf[:], in0=idxf[:], scalar1=float(npool * 128), scalar2=None, op0=ALU.add), [idxB], [idxB])
        V(_I("tensor_copy", out=idxi[:, 256:512], in_=idxf[:]), [idxB], [idxB])
        for b_ in VpB:
            pass
        sampB = KtB + VtB + KTsB + [KTnB, VnewB, VnB]

        def handoff(src, dst):
            evs = []
            for b_ in src:
                if b_.lw is not None:
                    evs.append(b_.lw)
                evs.extend(b_.rd)
            evs = _compact(evs)
            for d_ in dst:
                d_.rd = _compact(d_.rd + evs)
        V(_I("memset", lfn[:], 0.0), [], [lfnB])
        V(_I("memset", OT[:], 0.0), [], [OTB])

        def f1get():
            i = f1i[0] % 4; f1i[0] += 1
            return f1[i], f1B[i]

        def stget():
            i = stgi[0] % 4; stgi[0] += 1
            return stg[i], stgB[i]

        def wload(parts):
            i = wi[0] % NSLOT; wi[0] += 1
            for (c0, a, b, src) in parts:
                dst = wsl[i][:, c0:c0 + a * b].rearrange("p (a b) -> p a b", a=a)
                P.dma("gpsimd", _I("dma_start", out=dst, in_=src), wslB[i], True)
            return wsl[i], wslB[i]

        def win_cols(l, c0, w):
            return w_in[l][:, c0:c0 + w].rearrange("(kc p) n -> p kc n", p=128)

        def norm(l_g_off, N, out_bf):
            for kc in range(8):
                A(_I("activation", out=hT[:, kc, 0:N], in_=xT[:, kc, 0:N], func=AF.Square), [xTB[kc]], [hTB])
            bi = ps_get()
            for kc in range(8):
                T(_I("matmul", banks[bi][:, 0:N], lhsT=onesb, rhs=hT[:, kc, 0:N], start=(kc == 0), stop=(kc == 7)),
                  [hTB, cstbB], [bankB[bi]])
            r, rB = rstdT, rstdB
            V(_I("tensor_scalar", out=r[:, 0:N], in0=banks[bi][:, 0:N], scalar1=1.0 / D, scalar2=EPS, op0=ALU.mult, op1=ALU.add),
              [bankB[bi]], [rB])
            ps_rel(bi)
            A(_I("activation", out=r[:, 0:N], in_=r[:, 0:N], func=AF.Ln), [rB], [rB])
            A(_I("activation", out=r[:, 0:N], in_=r[:, 0:N], func=AF.Exp, scale=-0.5), [rB], [rB])
            return r, rB

        def mark(name):
            if stop == name:
                raise _Stop()
        try:
          for (g0, N, is_s) in passes:
              nt = N // 128
              xsrc = xs if is_s else xp[g0:g0 + N, :]
              for tt in range(nt):
                  for half in range(2):
                      s_, sB_ = stget()
                      P.dma("sync", _I("dma_start", out=s_[:], in_=xsrc[tt * 128:(tt + 1) * 128, half * 512:(half + 1) * 512]), sB_, True)
                      bi = ps_get()
                      for c in range(4):
                          T(_I("transpose", out=banks[bi][:, c * 128:(c + 1) * 128], in_=s_[:, c * 128:(c + 1) * 128], identity=identf),
                            [sB_, cstB], [bankB[bi]])
                      for c in range(4):
                          kc = half * 4 + c
                          V(_I("tensor_copy", out=xT[:, kc, tt * 128:(tt + 1) * 128], in_=banks[bi][:, c * 128:(c + 1) * 128]),
                            [bankB[bi]], [xTB[kc]])
                      ps_rel(bi)

              for l in range(DEPTH):
                  g1 = lambda kc: smp[:, SP_G1 + l * 8 + kc:SP_G1 + l * 8 + kc + 1]
                  g2 = lambda kc: smp[:, SP_G2 + l * 8 + kc:SP_G2 + l * 8 + kc + 1]
                  mark("load")
                  r, rB = norm(None, N, True)
                  for kc in range(8):
                      V(_I("scalar_tensor_tensor", out=hT[:, kc, 0:N], in0=xT[:, kc, 0:N], scalar=g1(kc), in1=r[:, 0:N],
                                                                op0=ALU.mult, op1=ALU.mult), [xTB[kc], rB, smpB], [hTB])

                  def proj_fm(wt, wB, k0, nk, kcols, act_fn, rhs_fn=None, rdB=None):
                      bi = ps_get()
                      for k in range(nk):
                          T(_I("matmul", banks[bi][:, 0:N], lhsT=kcols(k), rhs=(rhs_fn(k) if rhs_fn else hT[:, k, 0:N]),
                                                    start=(k == 0), stop=(k == nk - 1)), [wB, rdB or hTB], [bankB[bi]])
                      return bi

                  mark("N1")
                  uaw = (lambda cc: uaS[:, cc, :, 2:10]) if is_s else (lambda cc: ua[:, cc, 2:2 + N])
                  ubw = (lambda cc: ubS[:, cc, :, 30:38]) if is_s else (lambda cc: ub[:, cc, 30:30 + N])
                  v3 = (lambda ap: ap.rearrange("p (b t) -> p b t", t=8)) if is_s else (lambda ap: ap)
                  if is_s:
                      for (src, nrow, dstf) in ((sca, 2, lambda cc, gsz, g: uaS[:, cc, :, 0:2]),):
                          s_, sB_ = stget()
                          P.dma("sync", _I("dma_start", out=s_[0:32, 0:256], in_=sca[l]), sB_, True)
                          for cc in range(2):
                              bi = ps_get()
                              T(_I("transpose", out=banks[bi][:, 0:32], in_=s_[0:32, cc * 128:(cc + 1) * 128], identity=identf[0:32, 0:32]),
                                [sB_, cstB], [bankB[bi]])
                              V(_I("tensor_copy", out=uaS[:, cc, :, 0:2], in_=banks[bi][:, 0:32].rearrange("p (b t) -> p b t", t=2)),
                                [bankB[bi]], [uaB])
                              ps_rel(bi)
                      for g in range(4):
                          s_, sB_ = stget()
                          P.dma("sync", _I("dma_start", out=s_[0:120, 0:256], in_=scb[l][g * 120:(g + 1) * 120, :]), sB_, True)
                          for cc in range(2):
                              bi = ps_get()
                              T(_I("transpose", out=banks[bi][:, 0:120], in_=s_[0:120, cc * 128:(cc + 1) * 128], identity=identf[0:120, 0:120]),
                                [sB_, cstB], [bankB[bi]])
                              V(_I("tensor_copy", out=ubS[:, cc, g * 4:(g + 1) * 4, 0:30], in_=banks[bi][:, 0:120].rearrange("p (b t) -> p b t", t=30)),
                                [bankB[bi]], [ubB])
                              ps_rel(bi)
                  elif g0 == 0:
                      V(_I("memset", ua[:, :, 0:2], 0.0), [], [uaB])
                      V(_I("memset", ub[:, :, 0:30], 0.0), [], [ubB])
                  else:
                      V(_I("tensor_copy", out=ua[:, :, 0:2], in_=histA[:, l, :, :]), [histAB[l]], [uaB])
                      V(_I("tensor_copy", out=ub[:, :, 0:30], in_=histB_[:, l, :, :]), [histBB[l]], [ubB])

                  for (c0, w) in ((OFF_SB, 512), (OFF_SH, 512), (OFF_GB, 256)):
                      wt, wB = wload([(0, 8, w, win_cols(l, c0, w))])
                      wv = wt[:, 0:8 * w].rearrange("p (k n) -> p k n", k=8)
                      for oc in range(w // 128):
                          col = c0 + oc * 128
                          bi = proj_fm(wt, wB, 0, 8, lambda k, oc=oc, wv=wv: wv[:, k, oc * 128:(oc + 1) * 128], None)
                          pv = banks[bi][:, 0:N]
                          if col < OFF_SC:
                              cc = (col - OFF_SB) // 128
                              A(_I("activation", out=sbT[:, cc, 0:N], in_=pv, func=AF.Identity), [bankB[bi]], [sbTB])
                          elif col < OFF_SH:
                              cc = (col - OFF_SC) // 128
                              A(_I("activation", out=tmpA[:, cc, 0:N], in_=pv, func=AF.Identity), [bankB[bi]], [tmpAB])
                          elif col < OFF_GA:
                              cc = (col - OFF_SH) // 128
                              V(_I("tensor_tensor", out=uaw(cc), in0=v3(pv), in1=v3(tmpA[:, cc, 0:N]), op=ALU.mult),
                                [bankB[bi], tmpAB], [uaB])
                          elif col < OFF_GB:
                              cc = (col - OFF_GA) // 128
                              A(_I("activation", out=tmpA[:, cc, 0:N], in_=pv, func=AF.Identity), [bankB[bi]], [tmpAB])
                          else:
                              cc = (col - OFF_GB) // 128
                              t_, tB_ = f1get()
                              A(_I("activation", out=t_[:, 0:N], in_=pv, func=AF.Sigmoid), [bankB[bi]], [tB_])
                              V(_I("tensor_tensor", out=ubw(cc), in0=v3(t_[:, 0:N]), in1=v3(tmpA[:, cc, 0:N]), op=ALU.mult),
                                [tB_, tmpAB], [ubB])
                          ps_rel(bi)

                  mark("P1")
                  caw = lambda j, cc: smp[:, SP_CAW + l * 6 + j * 2 + cc:SP_CAW + l * 6 + j * 2 + cc + 1]
                  cbw = lambda j, cc: smp[:, SP_CBW + l * 62 + j * 2 + cc:SP_CBW + l * 62 + j * 2 + cc + 1]
                  spc = lambda off, cc: smp[:, off + l * 2 + cc:off + l * 2 + cc + 1]
                  uain = (lambda cc, j: uaS[:, cc, :, j:j + 8]) if is_s else (lambda cc, j: ua[:, cc, j:j + N])
                  ubin = (lambda cc, j: ubS[:, cc, :, j:j + 8]) if is_s else (lambda cc, j: ub[:, cc, j:j + N])
                  for cc in range(2):
                      za, zaB = f1get()
                      V(_I("tensor_scalar", out=v3(za[:, 0:N]), in0=uain(cc, 0), scalar1=caw(0, cc), scalar2=None, op0=ALU.mult),
                        [uaB, smpB], [zaB])
                      for j in (1, 2):
                          V(_I("scalar_tensor_tensor", out=v3(za[:, 0:N]), in0=uain(cc, j), scalar=caw(j, cc), in1=v3(za[:, 0:N]),
                                                                                op0=ALU.mult, op1=ALU.add), [uaB, smpB, zaB], [zaB])
                      V(_I("tensor_tensor", out=AinT[:, cc, 0:N], in0=za[:, 0:N], in1=sbT[:, cc, 0:N], op=ALU.mult),
                        [zaB, sbTB], [AinB])
                      V(_I("tensor_scalar", out=v3(zb[:, cc, 0:N]), in0=ubin(cc, 0), scalar1=cbw(0, cc), scalar2=spc(SP_CBB, cc),
                                                         op0=ALU.mult, op1=ALU.add), [ubB, smpB], [zbB])
                      for j in range(1, 31):
                          V(_I("scalar_tensor_tensor", out=v3(zb[:, cc, 0:N]), in0=ubin(cc, j), scalar=cbw(j, cc), in1=v3(zb[:, cc, 0:N]),
                                                                         op0=ALU.mult, op1=ALU.add), [ubB, smpB, zbB], [zbB])
                  mark("taps")
                  b1 = ps_get(); b2 = ps_get()
                  sq, sqB = f1get()
                  for cc in range(2):
                      T(_I("matmul", banks[b1][:, 0:N], lhsT=onesf, rhs=zb[:, cc, 0:N], start=(cc == 0), stop=(cc == 1)),
                        [zbB, cstB], [bankB[b1]])
                  for cc in range(2):
                      A(_I("activation", out=sq[:, 0:N], in_=zb[:, cc, 0:N], func=AF.Square), [zbB], [sqB])
                      T(_I("matmul", banks[b2][:, 0:N], lhsT=onesf, rhs=sq[:, 0:N], start=(cc == 0), stop=(cc == 1)),
                        [sqB, cstB], [bankB[b2]])
                  mark("lnmm")
                  mean, meanB = f1get(); rs, rsB = f1get()
                  V(_I("tensor_scalar", out=mean[:, 0:N], in0=banks[b1][:, 0:N], scalar1=1.0 / 256, scalar2=None, op0=ALU.mult), [bankB[b1]], [meanB])
                  V(_I("tensor_tensor", out=rs[:, 0:N], in0=mean[:, 0:N], in1=mean[:, 0:N], op=ALU.mult), [meanB], [rsB])
                  V(_I("scalar_tensor_tensor", out=rs[:, 0:N], in0=banks[b2][:, 0:N], scalar=1.0 / 256, in1=rs[:, 0:N], op0=ALU.mult, op1=ALU.subtract),
                    [bankB[b2], rsB], [rsB])
                  V(_I("tensor_scalar", out=rs[:, 0:N], in0=rs[:, 0:N], scalar1=EPS, scalar2=None, op0=ALU.add), [rsB], [rsB])
                  A(_I("activation", out=rs[:, 0:N], in_=rs[:, 0:N], func=AF.Ln), [rsB], [rsB])
                  A(_I("activation", out=rs[:, 0:N], in_=rs[:, 0:N], func=AF.Exp, scale=-0.5), [rsB], [rsB])
                  ps_rel(b1); ps_rel(b2)
                  mark("lnrs")
                  for cc in range(2):
                      V(_I("tensor_tensor", out=zb[:, cc, 0:N], in0=zb[:, cc, 0:N], in1=mean[:, 0:N], op=ALU.subtract), [zbB, meanB], [zbB])
                      V(_I("tensor_tensor", out=zb[:, cc, 0:N], in0=zb[:, cc, 0:N], in1=rs[:, 0:N], op=ALU.mult), [zbB, rsB], [zbB])
                      A(_I("activation", out=BinT[:, cc, 0:N], in_=zb[:, cc, 0:N], func=AF.Silu, scale=spc(SP_CNG, cc), bias=spc(SP_CNB, cc)),
                        [zbB, smpB], [BinB])
                  mark("silu")
                  if is_s:
                      for cc in range(2):
                          bi = ps_get()
                          c_, cB_ = f1get()
                          V(_I("tensor_copy", out=c_[:, 0:32].rearrange("p (b t) -> p b t", t=2), in_=uaS[:, cc, :, 8:10]), [uaB], [cB_])
                          T(_I("transpose", out=banks[bi][0:32, 0:128], in_=c_[:, 0:32], identity=identf), [cB_, cstB], [bankB[bi]])
                          s_, sB_ = stget()
                          V(_I("tensor_copy", out=s_[0:32, 0:128], in_=banks[bi][0:32, 0:128]), [bankB[bi]], [sB_])
                          ps_rel(bi)
                          P.dma("sync", _I("dma_start", out=ca_s[l][:, cc * 128:(cc + 1) * 128], in_=s_[0:32, 0:128]), sB_, False, is_output=True)
                          for g in range(4):
                              bi = ps_get()
                              c_, cB_ = f1get()
                              V(_I("tensor_copy", out=c_[:, 0:120].rearrange("p (b t) -> p b t", t=30), in_=ubS[:, cc, g * 4:(g + 1) * 4, 8:38]), [ubB], [cB_])
                              T(_I("transpose", out=banks[bi][0:120, 0:128], in_=c_[:, 0:120], identity=identf),
                                [cB_, cstB], [bankB[bi]])
                              s_, sB_ = stget()
                              V(_I("tensor_copy", out=s_[0:120, 0:128], in_=banks[bi][0:120, 0:128]), [bankB[bi]], [sB_])
                              ps_rel(bi)
                              P.dma("sync", _I("dma_start",
                                  out=cb_s[l][g * 4:(g + 1) * 4, :, cc * 128:(cc + 1) * 128].rearrange("b t c -> (b t) c"), in_=s_[0:120, 0:128]), sB_, False, is_output=True)
                  elif g0 + N < SEQ:
                      V(_I("tensor_copy", out=histA[:, l, :, :], in_=ua[:, :, N:N + 2]), [uaB], [histAB[l]])
                      V(_I("tensor_copy", out=histB_[:, l, :, :], in_=ub[:, :, N:N + 30]), [ubB], [histBB[l]])
                  else:
                      for cc in range(2):
                          for (buf_, bB_, nr, dst) in ((ua, uaB, 2, ca_p), (ub, ubB, 30, cb_p)):
                              bi = ps_get()
                              hh = 2 if nr == 2 else 30
                              T(_I("transpose", out=banks[bi][0:nr, 0:128], in_=buf_[:, cc, hh + N - nr:hh + N], identity=identf),
                                [bB_, cstB], [bankB[bi]])
                              s_, sB_ = stget()
                              V(_I("tensor_copy", out=s_[0:nr, 0:128], in_=banks[bi][0:nr, 0:128]), [bankB[bi]], [sB_])
                              ps_rel(bi)
                              P.dma("sync", _I("dma_start", out=dst[l][:, cc * 128:(cc + 1) * 128], in_=s_[0:nr, 0:128]),
                                    sB_, False, is_output=True)

                  mark("conv")
                  gb0 = g0 // 128
                  kvset = sampB if is_s else (KTB + VpB)
                  handoff(aTB + KTB + VpB + sampB, kvset)
                  if not is_s:
                      V(_I("memset", Vp[:, :, :, 64:65], 1.0), [], VpB)
                  if (not is_s) and g0 > 0:
                      P.dma("sync", _I("dma_start", out=KT[:, :, 0:g0], in_=kt_scr[l][:, :, 0:g0]), KTB[0], True, writes=KTB[1:gb0])
                      P.dma("sync", _I("dma_start", out=KVR[:, 8192:8192 + gb0 * 520], in_=vp_scr[l][:, 0:gb0, :].rearrange("p b e -> p (b e)")),
                            VpB[0], True, writes=VpB[1:gb0])
                      P.dma("sync", _I("dma_start", out=ctok[:, 0:gb0, :], in_=c_scr[l][:, 0:gb0, :]), ctokB, True)
                  if g0 == 0 and not is_s:
                      V(_I("memset", totA[:, l, :], 0.0), [], [totBs[l]])
                  kout = k_s if is_s else k_p; vout = v_s if is_s else v_p; lfout = lf_s if is_s else lf_p
                  for (c0, w, kind) in ((OFF_K, 512, "k"), (OFF_V, 512, "v"), (OFF_F, 8, "f")):
                      mark("pre_" + kind)
                      wt, wB = wload([(0, 8, w, win_cols(l, c0, w))])
                      wv = wt[:, 0:8 * w].rearrange("p (k n) -> p k n", k=8)
                      for tt in range(nt):
                          gb = gb0 + tt
                          bi = ps_get()
                          for k in range(8):
                              T(_I("matmul", banks[bi][:, 0:w], lhsT=hT[:, k, tt * 128:(tt + 1) * 128], rhs=wv[:, k, :],
                                                                      start=(k == 0), stop=(k == 7)), [wB, hTB], [bankB[bi]])
                          rows = slice(g0 + tt * 128, g0 + (tt + 1) * 128)
                          if kind in ("k", "v"):
                              s_, sB_ = stget()
                              A(_I("activation", out=s_[:], in_=banks[bi][:], func=AF.Identity), [bankB[bi]], [sB_])
                              dst = kout if kind == "k" else vout
                              if KDBG >= 2: P.dma("sync", _I("dma_start", out=dst[l][rows, :], in_=s_[:]), sB_, False, is_output=True)
                          if kind == "k":
                              if KDBG >= 3: V(_I("tensor_copy", out=kbf[:], in_=s_[:]), [sB_], [kbfB])
                              ps_rel(bi)
                              tb_, tbB_ = tget()
                              pbf = tb_[:, 0:512]
                              for j in range(4):
                                  if KDBG >= 4: T(_I("transpose", out=pbf[:, j * 128:(j + 1) * 128], in_=kbf[:, j * 128:(j + 1) * 128], identity=identb),
                                    [kbfB, cstbB], [tbB_])
                              if is_s:
                                  V(_I("tensor_copy", out=KTn[:], in_=pbf.rearrange("p (j n) -> p j n", j=4)), [tbB_], [KTnB])
                              elif KDBG >= 5:
                                  V(_I("tensor_copy", out=KT[:, :, gb * 128:(gb + 1) * 128], in_=pbf.rearrange("p (j n) -> p j n", j=4)),
                                    [tbB_], [KTB[gb]])
                          elif kind == "v":
                              if is_s:
                                  V(_I("tensor_copy", out=Vnew, in_=s_[:]), [sB_], [VnewB])
                              else:
                                  V(_I("tensor_copy", out=Vp[:, gb, :, 0:64], in_=s_[:].rearrange("p (h d) -> p h d", h=8)),
                                    [sB_], [VpB[gb]])
                              ps_rel(bi)
                          else:
                              bfb = smp[:, SP_BF + l * 8:SP_BF + l * 8 + 8]
                              lt = lfnew if is_s else lft; ltB = lfnewB if is_s else lftB
                              V(_I("tensor_tensor", out=sm8[0][:], in0=banks[bi][:, 0:8], in1=bfb, op=ALU.add), [bankB[bi], smpB], [sm8B[0]])
                              ps_rel(bi)
                              A(_I("activation", out=sm8[0][:], in_=sm8[0][:], func=AF.Exp, scale=-1.0), [sm8B[0]], [sm8B[0]])
                              A(_I("activation", out=sm8[0][:], in_=sm8[0][:], func=AF.Ln, bias=cst[:, C_ONE:C_ONE + 1], scale=1.0), [sm8B[0], cstB], [sm8B[0]])
                              V(_I("tensor_scalar", out=lt[:], in0=sm8[0][:], scalar1=-1.0, scalar2=None, op0=ALU.mult), [sm8B[0]], [ltB])
                              P.dma("sync", _I("dma_start", out=lfout[l][rows, :], in_=lt[:]), ltB, False, is_output=True)
                              if not is_s:
                                  b1 = ps_get()
                                  T(_I("matmul", banks[b1][:, 0:8], lhsT=Lf, rhs=lft[:], start=True, stop=True), [lftB, cstB], [bankB[b1]])
                                  T(_I("matmul", banks[b1][:, 8:16], lhsT=onesf, rhs=lft[:], start=True, stop=True), [lftB, cstB], [bankB[b1]])
                                  V(_I("tensor_tensor", out=ctok[:, gb, :], in0=banks[b1][:, 0:8], in1=totA[:, l, :], op=ALU.add), [bankB[b1], totBs[l]], [ctokB])
                                  V(_I("tensor_tensor", out=totA[:, l, :], in0=banks[b1][:, 8:16], in1=totA[:, l, :], op=ALU.add), [bankB[b1], totBs[l]], [totBs[l]])
                                  ps_rel(b1)
                  mark("kvf")
                  if (not is_s) and g0 + N < SEQ:
                      P.dma("sync", _I("dma_start", out=kt_scr[l][:, :, g0:g0 + N], in_=KT[:, :, g0:g0 + N]), KTB[gb0], False, reads=KTB[gb0 + 1:gb0 + nt])
                      P.dma("sync", _I("dma_start", out=vp_scr[l][:, gb0:gb0 + nt, :].rearrange("p b e -> p (b e)"), in_=KVR[:, 8192 + gb0 * 520:8192 + (gb0 + nt) * 520]),
                            VpB[gb0], False, reads=VpB[gb0 + 1:gb0 + nt])
                      P.dma("sync", _I("dma_start", out=c_scr[l][:, gb0:gb0 + nt, :], in_=ctok[:, gb0:gb0 + nt, :]), ctokB, False)

                  mark("P2")
                  wt, wB = wload([(0, 8, 512, win_cols(l, OFF_Q, 512))])
                  wv = wt[:, 0:4096].rearrange("p (k n) -> p k n", k=8)
                  for j in range(4):
                      bi = proj_fm(wt, wB, 0, 8, lambda k, j=j, wv=wv: wv[:, k, j * 128:(j + 1) * 128], None)
                      A(_I("activation", out=QT[:, j, 0:N], in_=banks[bi][:, 0:N], func=AF.Identity, scale=0.125), [bankB[bi]], [QTB])
                      ps_rel(bi)

                  mark("P3")
                  if not is_s:
                      nkb = (g0 + N) // 128
                      bc_ = ps_get()
                      T(_I("matmul", banks[bc_][:, 0:8], lhsT=s127, rhs=ctok[:, nkb - 1, :], start=True, stop=True), [ctokB, cstB], [bankB[bc_]])
                      V(_I("tensor_copy", out=sm8[1][:], in_=banks[bc_][:, 0:8]), [bankB[bc_]], [sm8B[1]])
                      ps_rel(bc_)
                      V(_I("tensor_tensor", out=biasT[:, 0:nkb, :], in0=sm8[1][:].unsqueeze(1).to_broadcast([128, nkb, 8]), in1=ctok[:, 0:nkb, :], op=ALU.subtract),
                        [sm8B[1], ctokB], [biasTB])
                      for h in range(NH):
                          j = h // 2; r0 = 64 * (h % 2)
                          bo = ps_get()
                          pend = None

                          def s_mm(kb):
                              cl = max(0, 128 * kb - g0)
                              bs = ps_get()
                              T(_I("matmul", banks[bs][:, cl:N], lhsT=KT[r0:r0 + 64, j, kb * 128:(kb + 1) * 128], rhs=QT[r0:r0 + 64, j, cl:N], start=True, stop=True),
                                [KTB[kb], QTB], [bankB[bs]])
                              return bs, cl
                          cur = s_mm(0)
                          for kb in range(nkb):
                              nxt = s_mm(kb + 1) if kb + 1 < nkb else None
                              bs, cl = cur
                              pi = (h * 16 + kb) % 3
                              A(_I("activation", out=pTt[pi][:, cl:N], in_=banks[bs][:, cl:N], func=AF.Exp, bias=biasT[:, kb, h:h + 1], scale=1.0),
                                [bankB[bs], biasTB], [pTB[pi]])
                              ps_rel(bs)
                              if 128 * kb >= g0:
                                  V(_I("tensor_tensor", out=pTt[pi][:, cl:cl + 128], in0=pTt[pi][:, cl:cl + 128], in1=trib, op=ALU.mult),
                                    [pTB[pi], cstbB], [pTB[pi]])
                              T(_I("matmul", banks[bo][0:65, cl:N], lhsT=Vp[:, kb, h, :], rhs=pTt[pi][:, cl:N], start=(kb == 0), stop=(kb == nkb - 1)),
                                [VpB[kb], pTB[pi]], [bankB[bo]])
                              cur = nxt
                          V(_I("reciprocal", out=rl[64:65, 0:N], in_=banks[bo][64:65, 0:N]), [bankB[bo]], [rlB])
                          bb = ps_get()
                          T(_I("matmul", banks[bb][0:64, 0:N], lhsT=onesf[64:65, 0:64], rhs=rl[64:65, 0:N], start=True, stop=True), [rlB, cstB], [bankB[bb]])
                          t_, tB_ = f1get()
                          A(_I("activation", out=t_[0:64, 0:N], in_=banks[bb][0:64, 0:N], func=AF.Identity), [bankB[bb]], [tB_])
                          ps_rel(bb)
                          V(_I("tensor_tensor", out=OT[0:64, h, 0:N], in0=banks[bo][0:64, 0:N], in1=t_[0:64, 0:N], op=ALU.mult), [bankB[bo], tB_], [OTB])
                          ps_rel(bo)
                  else:
                      V(_I("tensor_tensor", out=lfblk[:].rearrange("p (b h) -> p b h", b=16), in0=lfnew[:].unsqueeze(1).to_broadcast([128, 16, 8]),
                                      in1=bmask.unsqueeze(2).to_broadcast([128, 16, 8]), op=ALU.mult), [lfnewB, cstB], [lfblkB])
                      bx = ps_get()
                      T(_I("matmul", banks[bx][:, 0:128], lhsT=onesf, rhs=lfblk[:], start=True, stop=True), [lfblkB, cstB], [bankB[bx]])
                      T(_I("matmul", banks[bx][:, 128:136], lhsT=sufmat, rhs=lfnew[:], start=True, stop=True), [lfnewB, cstB], [bankB[bx]])
                      V(_I("tensor_copy", out=NTs[:], in_=banks[bx][:, 0:136]), [bankB[bx]], [NTsB])
                      ps_rel(bx)
                      V(_I("memset", Qbd[:], 0.0), [], [QbdB])
                      for j in range(4):
                          V(_I("tensor_copy", out=Qbd[0:64, j, :, 0:8], in_=QT[0:64, j, 0:128].rearrange("p (b t) -> p b t", t=8)), [QTB], [QbdB])
                          V(_I("tensor_copy", out=Qbd[64:128, j, :, 8:16], in_=QT[64:128, j, 0:128].rearrange("p (b t) -> p b t", t=8)), [QTB], [QbdB])
                      def sa_bias(b):
                          for pg in range(NPAGES):
                              P.dma("gpsimd", _I("indirect_dma_start",
                                  out=lfS[:, pg, :], out_offset=None, in_=cache_f,
                                  in_offset=bass.IndirectOffsetOnAxis(ap=idxi[:, l * 256 + b * 16 + pg:l * 256 + b * 16 + pg + 1], axis=0)), lfSB, True, reads=[idxB])
                          bw = ps_get(); bt = ps_get()
                          lfS2 = lfS[:].rearrange("p a h -> p (a h)")
                          T(_I("matmul", banks[bw][:, 0:128], lhsT=Lf, rhs=lfS2, start=True, stop=True), [lfSB, cstB], [bankB[bw]])
                          T(_I("matmul", banks[bt][:, 0:128], lhsT=onesf, rhs=lfS2, start=True, stop=True), [lfSB, cstB], [bankB[bt]])
                          V(_I("tensor_copy", out=totS[:].rearrange("p a h -> p (a h)"), in_=banks[bt][:, 0:128]), [bankB[bt]], [totSB])
                          ps_rel(bt)
                          V(_I("tensor_tensor", out=totS[:, 15, :], in0=totS[:, 15, :], in1=NTs[:, b * 8:(b + 1) * 8], op=ALU.add), [totSB, NTsB], [totSB])
                          for pg in range(14, -1, -1):
                              V(_I("tensor_tensor", out=totS[:, pg, :], in0=totS[:, pg, :], in1=totS[:, pg + 1, :], op=ALU.add), [totSB], [totSB])
                          V(_I("tensor_tensor", out=bpast[:].rearrange("p a h -> p (a h)"), in0=totS[:].rearrange("p a h -> p (a h)"), in1=banks[bw][:, 0:128], op=ALU.subtract),
                            [totSB, bankB[bw]], [bpastB])
                          ps_rel(bw)

                      def sa_A(b, qd):
                          sl = (b * 4 + qd) % 2
                          for pg in range(4):
                              col = b * 16 + qd * 4 + pg
                              P.dma("gpsimd", _I("indirect_dma_start",
                                  out=Kt[sl][:, pg, :], out_offset=None, in_=cache_k,
                                  in_offset=bass.IndirectOffsetOnAxis(ap=idxi[:, l * 256 + col:l * 256 + col + 1], axis=0)), KtB[sl], True, reads=[idxB])
                              P.dma("gpsimd", _I("indirect_dma_start",
                                  out=Vt[sl][:, pg, :], out_offset=None, in_=cache_v,
                                  in_offset=bass.IndirectOffsetOnAxis(ap=idxi[:, l * 256 + col:l * 256 + col + 1], axis=0)), VtB[sl], True, reads=[idxB])
                          for pp in range(2):
                              tb_, tbB_ = tget()
                              pbf = tb_[:].rearrange("p (j n) -> p j n", j=4)
                              for p2 in range(2):
                                  pg = pp * 2 + p2
                                  for j in range(4):
                                      T(_I("transpose", out=pbf[:, j, p2 * 128:(p2 + 1) * 128], in_=Kt[sl][:, pg, j * 128:(j + 1) * 128], identity=identb),
                                        [KtB[sl], cstbB], [tbB_])
                              V(_I("tensor_copy", out=KTs[sl][:, :, pp * 256:(pp + 1) * 256], in_=pbf), [tbB_], [KTsB[sl]])
                          bs = ps_get()
                          for pg in range(4):
                              for j in range(4):
                                  T(_I("matmul", banks[bs][:, pg * 64 + j * 16:pg * 64 + j * 16 + 16], lhsT=KTs[sl][:, j, pg * 128:(pg + 1) * 128],
                                       rhs=Qbd[:, j, b, :], start=True, stop=True), [KTsB[sl], QbdB], [bankB[bs]])
                          V(_I("tensor_tensor", out=spS[:].rearrange("p (a h t) -> p a h t", a=4, h=8), in0=banks[bs][:, 0:256].rearrange("p (a h t) -> p a h t", a=4, h=8),
                               in1=bpast[:, qd * 4:(qd + 1) * 4, :].unsqueeze(3).to_broadcast([128, 4, 8, 8]), op=ALU.add), [bankB[bs], bpastB], [spSB])
                          ps_rel(bs)
                          A(_I("activation", out=pTs2[sl][:], in_=spS[:], func=AF.Exp), [spSB], [pTs2B[sl]])

                      def sa_B(b, qd):
                          sl = (b * 4 + qd) % 2
                          bo = ps_get()
                          for h in range(NH):
                              for pg in range(4):
                                  T(_I("matmul", banks[bo][0:64, h * 8:(h + 1) * 8], lhsT=Vt[sl][:, pg, h * 64:(h + 1) * 64], rhs=pTs2[sl][:, pg * 64 + h * 8:pg * 64 + h * 8 + 8],
                                       start=(pg == 0), stop=(pg == 3)), [VtB[sl], pTs2B[sl]], [bankB[bo]])
                          for pg in range(4):
                              T(_I("matmul", banks[bo][0:64, 64:128], lhsT=onesb[:, 0:64], rhs=pTs2[sl][:, pg * 64:(pg + 1) * 64], start=(pg == 0), stop=(pg == 3)),
                                [pTs2B[sl], cstbB], [bankB[bo]])
                          if qd == 0:
                              V(_I("tensor_copy", out=Oacc[:], in_=banks[bo][0:64, 0:128]), [bankB[bo]], [OaccB])
                          else:
                              V(_I("tensor_tensor", out=Oacc[:], in0=Oacc[:], in1=banks[bo][0:64, 0:128], op=ALU.add), [bankB[bo], OaccB], [OaccB])
                          ps_rel(bo)

                      def sa_new(b):
                          bo = ps_get(); bs = ps_get()
                          for j in range(4):
                              T(_I("matmul", banks[bs][:, j * 16:(j + 1) * 16], lhsT=KTn[:, j, :], rhs=Qbd[:, j, b, :],
                                   start=True, stop=True), [KTnB, QbdB], [bankB[bs]])
                          V(_I("tensor_tensor", out=spN[:].rearrange("p (h t) -> p h t", h=8), in0=banks[bs][:, 0:64].rearrange("p (h t) -> p h t", h=8),
                               in1=NTs[:, 128:136].unsqueeze(2).to_broadcast([128, 8, 8]), op=ALU.add), [bankB[bs], NTsB], [spNB])
                          ps_rel(bs)
                          A(_I("activation", out=pTn[:], in_=spN[:], func=AF.Exp), [spNB], [pTnB])
                          V(_I("tensor_tensor", out=pTn[:].rearrange("p (h t) -> p h t", h=8), in0=pTn[:].rearrange("p (h t) -> p h t", h=8),
                               in1=maskNb[:, b * 8:(b + 1) * 8].unsqueeze(1).to_broadcast([128, 8, 8]), op=ALU.mult), [pTnB, cstbB], [pTnB])
                          for h in range(NH):
                              T(_I("matmul", banks[bo][0:64, h * 8:(h + 1) * 8], lhsT=Vnew[:, h * 64:(h + 1) * 64], rhs=pTn[:, h * 8:(h + 1) * 8], start=True, stop=True),
                                [VnewB, pTnB], [bankB[bo]])
                          T(_I("matmul", banks[bo][0:64, 64:128], lhsT=onesb[:, 0:64], rhs=pTn[:, 0:64], start=True, stop=True), [pTnB, cstbB], [bankB[bo]])
                          V(_I("tensor_tensor", out=Oacc[:], in0=Oacc[:], in1=banks[bo][0:64, 0:128], op=ALU.add), [bankB[bo], OaccB], [OaccB])
                          ps_rel(bo)
                          V(_I("reciprocal", out=rl[0:64, 0:64], in_=Oacc[:, 64:128]), [OaccB], [rlB])
                          V(_I("tensor_tensor", out=OT[0:64, :, b * 8:(b + 1) * 8], in0=Oacc[:, 0:64].rearrange("p (h t) -> p h t", h=8),
                               in1=rl[0:64, 0:64].rearrange("p (h t) -> p h t", h=8), op=ALU.mult), [OaccB, rlB], [OTB])

                      units = [(b, qd) for b in range(NSEQ) for qd in range(4)]
                      sa_bias(0)
                      sa_A(0, 0)
                      for ui, (b, qd) in enumerate(units):
                          if ui + 1 < len(units):
                              nb, nq = units[ui + 1]
                              if nq == 0:
                                  sa_bias(nb)
                              sa_A(nb, nq)
                          sa_B(b, qd)
                          if qd == 3:
                              sa_new(b)

                  mark("attn")
                  wcv = w_c[l].rearrange("(h d) n -> d h n", d=64)
                  for oc in range(8):
                      cs = slice(oc * 128, (oc + 1) * 128)
                      wg_, wgB = wload([(br * 1024, 8, 128, win_cols(l, OFF_GATE + br * 1024 + oc * 128, 128)) for br in range(3)])
                      wo_, woB = wload([(0, 2, 128, w_a[l][:, cs].rearrange("(c p) n -> p c n", p=128)),
                                        (256, 2, 128, w_b[l][:, cs].rearrange("(c p) n -> p c n", p=128))])
                      iS = (wi[0] - 1) % NSLOT
                      P.dma("gpsimd", _I("dma_start", out=wsl[iS][0:64, 512:1536].rearrange("p (h n) -> p h n", h=8), in_=wcv[:, :, cs]), wslB[iS], True)
                      sg = []
                      for br in range(3):
                          gv = wg_[:, br * 1024:(br + 1) * 1024].rearrange("p (k n) -> p k n", k=8)
                          bi = proj_fm(wg_, wgB, 0, 8, lambda k, gv=gv: gv[:, k, :], None)
                          t_, tB_ = f1get()
                          A(_I("activation", out=t_[:, 0:N], in_=banks[bi][:, 0:N], func=AF.Sigmoid), [bankB[bi]], [tB_])
                          ps_rel(bi)
                          sg.append((t_, tB_))
                      wav = wo_[:, 0:256].rearrange("p (c n) -> p c n", c=2); wbv = wo_[:, 256:512].rearrange("p (c n) -> p c n", c=2)
                      wcs = wo_[0:64, 512:1536].rearrange("p (h n) -> p h n", h=8)
                      bi = proj_fm(wo_, woB, 0, 2, lambda k: wav[:, k, :], None, rhs_fn=lambda k: AinT[:, k, 0:N], rdB=AinB)
                      V(_I("tensor_tensor", out=sg[0][0][:, 0:N], in0=banks[bi][:, 0:N], in1=sg[0][0][:, 0:N], op=ALU.mult), [bankB[bi], sg[0][1]], [sg[0][1]])
                      ps_rel(bi)
                      bi = proj_fm(wo_, woB, 0, 2, lambda k: wbv[:, k, :], None, rhs_fn=lambda k: BinT[:, k, 0:N], rdB=BinB)
                      V(_I("tensor_tensor", out=sg[1][0][:, 0:N], in0=banks[bi][:, 0:N], in1=sg[1][0][:, 0:N], op=ALU.mult), [bankB[bi], sg[1][1]], [sg[1][1]])
                      ps_rel(bi)
                      bi = proj_fm(wo_, woB, 0, 8, lambda k: wcs[:, k, :], None, rhs_fn=lambda k: OT[0:64, k, 0:N], rdB=OTB)
                      V(_I("tensor_tensor", out=sg[2][0][:, 0:N], in0=banks[bi][:, 0:N], in1=sg[2][0][:, 0:N], op=ALU.mult), [bankB[bi], sg[2][1]], [sg[2][1]])
                      ps_rel(bi)
                      V(_I("tensor_tensor", out=sg[0][0][:, 0:N], in0=sg[0][0][:, 0:N], in1=sg[1][0][:, 0:N], op=ALU.add), [sg[0][1], sg[1][1]], [sg[0][1]])
                      V(_I("tensor_tensor", out=mrgT[:, oc, 0:N], in0=sg[0][0][:, 0:N], in1=sg[2][0][:, 0:N], op=ALU.add), [sg[0][1], sg[2][1]], [mrgB])

                  mark("P4")
                  for og in range(2):
                      wt, wB = wload([(0, 8, 512, w_o[l][:, og * 512:(og + 1) * 512].rearrange("(kc p) n -> p kc n", p=128))])
                      wv = wt[:, 0:4096].rearrange("p (k n) -> p k n", k=8)
                      for o4 in range(4):
                          oc = og * 4 + o4
                          bi = proj_fm(wt, wB, 0, 8, lambda k, o4=o4, wv=wv: wv[:, k, o4 * 128:(o4 + 1) * 128], None, rhs_fn=lambda k: mrgT[:, k, 0:N], rdB=mrgB)
                          V(_I("tensor_tensor", out=xT[:, oc, 0:N], in0=xT[:, oc, 0:N], in1=banks[bi][:, 0:N], op=ALU.add), [bankB[bi], xTB[oc]], [xTB[oc]])
                          ps_rel(bi)

                  mark("Wo")
                  r, rB = norm(None, N, True)
                  for kc in range(8):
                      V(_I("scalar_tensor_tensor", out=hT[:, kc, 0:N], in0=xT[:, kc, 0:N], scalar=g2(kc), in1=r[:, 0:N],
                                                                op0=ALU.mult, op1=ALU.mult), [xTB[kc], rB, smpB], [hTB])
                  handoff(KTB + VpB + sampB, aTB)
                  for hg in range(11):
                      wg_, wgB = wload([(0, 8, 256, w_g[l][:, hg * 256:(hg + 1) * 256].rearrange("(kc p) n -> p kc n", p=128)),
                                        (2048, 8, 256, w_u[l][:, hg * 256:(hg + 1) * 256].rearrange("(kc p) n -> p kc n", p=128))])
                      gv = wg_[:, 0:2048].rearrange("p (k n) -> p k n", k=8); uv = wg_[:, 2048:4096].rearrange("p (k n) -> p k n", k=8)
                      for h2 in range(2):
                          hi = hg * 2 + h2
                          b1 = proj_fm(wg_, wgB, 0, 8, lambda k, h2=h2, gv=gv: gv[:, k, h2 * 128:(h2 + 1) * 128], None)
                          t_, tB_ = f1get()
                          A(_I("activation", out=t_[:, 0:N], in_=banks[b1][:, 0:N], func=AF.Silu), [bankB[b1]], [tB_])
                          ps_rel(b1)
                          b2 = proj_fm(wg_, wgB, 0, 8, lambda k, h2=h2, uv=uv: uv[:, k, h2 * 128:(h2 + 1) * 128], None)
                          V(_I("tensor_tensor", out=aT[:, hi, 0:N], in0=banks[b2][:, 0:N], in1=t_[:, 0:N], op=ALU.mult), [bankB[b2], tB_], [aTB[hi]])
                          ps_rel(b2)
                  for oc in range(8):
                      wt, wB = wload([(0, HC, 128, w_d[l][:, oc * 128:(oc + 1) * 128].rearrange("(hc p) n -> p hc n", p=128))])
                      wv = wt[:, 0:HC * 128].rearrange("p (k n) -> p k n", k=HC)
                      bi = ps_get()
                      for k in range(HC):
                          T(_I("matmul", banks[bi][:, 0:N], lhsT=wv[:, k, :], rhs=aT[:, k, 0:N], start=(k == 0), stop=(k == HC - 1)), [wB, aTB[k]], [bankB[bi]])
                      V(_I("tensor_tensor", out=xT[:, oc, 0:N], in0=xT[:, oc, 0:N], in1=banks[bi][:, 0:N], op=ALU.add), [bankB[bi], xTB[oc]], [xTB[oc]])
                      ps_rel(bi)

              r, rB = norm(None, N, True)
              yout = y_s if is_s else y_p
              for tt in range(nt):
                  for half in range(2):
                      bi = ps_get()
                      for c in range(4):
                          kc = half * 4 + c
                          t_, tB_ = f1get()
                          V(_I("scalar_tensor_tensor", out=t_[:, 0:128], in0=xT[:, kc, tt * 128:(tt + 1) * 128], scalar=smp[:, SP_GF + kc:SP_GF + kc + 1],
                                                                                  in1=r[:, tt * 128:(tt + 1) * 128], op0=ALU.mult, op1=ALU.mult), [xTB[kc], rB, smpB], [tB_])
                          T(_I("transpose", out=banks[bi][:, c * 128:(c + 1) * 128], in_=t_[:, 0:128], identity=identf), [tB_, cstB], [bankB[bi]])
                      s_, sB_ = stget()
                      A(_I("activation", out=s_[:], in_=banks[bi][:], func=AF.Identity), [bankB[bi]], [sB_])
                      ps_rel(bi)
                      rows = slice(g0 + tt * 128, g0 + (tt + 1) * 128) if not is_s else slice(tt * 128, (tt + 1) * 128)
                      P.dma("sync", _I("dma_start", out=yout[rows, half * 512:(half + 1) * 512], in_=s_[:]), sB_, False, is_output=True)

        except _Stop:
            pass
        P.finish()
    return nc


def _consts():
    c = np.zeros((128, C_N), np.float32)
    c[:, C_ID:C_ID + 128] = np.eye(128, dtype=np.float32)
    c[:, C_ONE:C_ONE + 128] = 1.0
    c[:, C_L:C_L + 128] = np.triu(np.ones((128, 128), np.float32))
    c[127, C_S127:C_S127 + 128] = 1.0
    c[7, C_S7:C_S7 + 128] = 1.0
    c[:, C_IOTA] = np.arange(128, dtype=np.float32)
    sidx = np.arange(128)
    for b in range(16):
        for t in range(8):
            c[:, C_MASKN + b * 8 + t] = ((sidx // 8 == b) & (sidx % 8 <= t)).astype(np.float32)
    c[:, C_SUF:C_SUF + 128] = ((sidx[:, None] // 8 == sidx[None, :] // 8) & (sidx[:, None] % 8 > sidx[None, :] % 8)).astype(np.float32)
    c[:, C_BM:C_BM + 16] = (sidx[:, None] // 8 == np.arange(16)[None, :]).astype(np.float32)
    return c


def kernel(x_prompt, x_sample, cache_k, cache_v, cache_logf, state_conv_a, state_conv_b, page_table,
           norm1_g, w_in, b_f, conv_a_w, conv_b_w, conv_b_bias, cf_norm_g, cf_norm_b,
           w_a_out, w_b_out, w_c_out, w_o, norm2_g, w_ffn_gate, w_ffn_up, w_ffn_down, final_norm_g):
    f = lambda a: np.ascontiguousarray(np.asarray(a))
    sp = np.zeros((128, SP_N), np.float32)
    sp[:, SP_G1:SP_G1 + 16] = np.asarray(norm1_g).reshape(2, 8, 128).transpose(2, 0, 1).reshape(128, 16)
    sp[:, SP_G2:SP_G2 + 16] = np.asarray(norm2_g).reshape(2, 8, 128).transpose(2, 0, 1).reshape(128, 16)
    sp[:, SP_GF:SP_GF + 8] = np.asarray(final_norm_g).reshape(8, 128).T
    sp[:, SP_CAW:SP_CAW + 12] = np.asarray(conv_a_w).reshape(2, 3, 2, 128).transpose(3, 0, 1, 2).reshape(128, 12)
    sp[:, SP_CBW:SP_CBW + 124] = np.asarray(conv_b_w).reshape(2, 31, 2, 128).transpose(3, 0, 1, 2).reshape(128, 124)
    sp[:, SP_CBB:SP_CBB + 4] = np.asarray(conv_b_bias).reshape(2, 2, 128).transpose(2, 0, 1).reshape(128, 4)
    sp[:, SP_CNG:SP_CNG + 4] = np.asarray(cf_norm_g).reshape(2, 2, 128).transpose(2, 0, 1).reshape(128, 4)
    sp[:, SP_CNB:SP_CNB + 4] = np.asarray(cf_norm_b).reshape(2, 2, 128).transpose(2, 0, 1).reshape(128, 4)
    sp[:, SP_BF:SP_BF + 16] = np.broadcast_to(np.asarray(b_f).reshape(1, 16), (128, 16))
    consts = _consts()
    ck = f(cache_k).reshape(DEPTH * NPOOL * 128, 512); cv = f(cache_v).reshape(DEPTH * NPOOL * 128, 512)
    cf = f(cache_logf).reshape(DEPTH * NPOOL * 128, 8)
    shared = dict(cache_k=ck, cache_v=cv, cache_f=cf, smallp=sp, consts=consts, w_in=f(w_in), w_a=f(w_a_out), w_b=f(w_b_out),
                  w_c=f(w_c_out), w_o=f(w_o), w_g=f(w_ffn_gate), w_u=f(w_ffn_up), w_d=f(w_ffn_down))
    xpn = np.asarray(x_prompt); xsn = np.asarray(x_sample)
    sca = np.asarray(state_conv_a); scb = np.asarray(state_conv_b); pt = np.asarray(page_table)
    in_maps = []
    for c in range(NCORES):
        sq = slice(c * NSEQ, (c + 1) * NSEQ)
        m = dict(shared)
        m["xp"] = f(xpn[c]); m["xs"] = f(xsn[sq].reshape(128, D))
        m["sca"] = f(sca[:, sq].reshape(DEPTH, NSEQ * 2, 256)); m["scb"] = f(scb[:, sq].reshape(DEPTH, NSEQ * 30, 256))
        m["ptb"] = f(pt[sq].reshape(1, NSEQ * NPAGES).astype(np.int32))
        in_maps.append(m)
    nc = build_nc()
    res = run_bass_kernel_spmd(nc, in_maps, core_ids=list(range(NCORES)))
    R = res.results
    st = lambda k: np.stack([np.asarray(r[k]) for r in R])
    y_prompt = st("y_p").reshape(8, SEQ, D)
    y_sample = st("y_s").reshape(128, DSEQ, D)
    k_prompt = st("k_p").transpose(1, 0, 2, 3).reshape(DEPTH, 8, 16, 128, NH, DH)
    v_prompt = st("v_p").transpose(1, 0, 2, 3).reshape(DEPTH, 8, 16, 128, NH, DH)
    logf_prompt = st("lf_p").transpose(1, 0, 2, 3).reshape(DEPTH, 8, 16, 128, NH)
    conv_a_prompt = st("ca_p").transpose(1, 0, 2, 3)
    conv_b_prompt = st("cb_p").transpose(1, 0, 2, 3)
    k_sample = st("k_s").transpose(1, 0, 2, 3).reshape(DEPTH, 128, DSEQ, NH, DH)
    v_sample = st("v_s").transpose(1, 0, 2, 3).reshape(DEPTH, 128, DSEQ, NH, DH)
    logf_sample = st("lf_s").transpose(1, 0, 2, 3).reshape(DEPTH, 128, DSEQ, NH)
    conv_a_sample = st("ca_s").transpose(1, 0, 2, 3).reshape(DEPTH, 128, 2, 256)
    conv_b_sample = st("cb_s").transpose(1, 0, 2, 3, 4).reshape(DEPTH, 128, 30, 256)
    outs = (y_prompt, y_sample, k_prompt, v_prompt, logf_prompt, conv_a_prompt, conv_b_prompt,
            k_sample, v_sample, logf_sample, conv_a_sample, conv_b_sample)
    return tuple(np.ascontiguousarray(o, dtype=np.float32) for o in outs)
```

```python
import os
import numpy as np
from contextlib import ExitStack
import concourse.bass as bass
import concourse.mybir as mybir
from concourse.bass_utils import run_bass_kernel_spmd

F32 = mybir.dt.float32
BF16 = mybir.dt.bfloat16
I32 = mybir.dt.int32
AF = mybir.ActivationFunctionType
ALU = mybir.AluOpType
ENGS = ["tensor", "vector", "scalar", "gpsimd", "sync"]

NCORES = 8
D = 1024; KC = 8; NH = 8; DH = 64; PT = 5896; DEPTH = 2
OFF_Q = 0; OFF_K = 512; OFF_V = 1024; OFF_F = 1536; OFF_SB = 1544; OFF_SC = 1800; OFF_SH = 2056
OFF_GA = 2312; OFF_GB = 2568; OFF_GATE = 2824
FFN = 2816; HC = 22
SEQ = 2048; NSEQ = 16; DSEQ = 8; NPAGES = 16; NPOOL = 2560
EPS = 1e-6
PASSES = [(0, 512, False), (512, 512, False), (1024, 512, False), (1536, 512, False), (0, 128, True)]
SP_G1 = 0; SP_G2 = 16; SP_GF = 32; SP_CAW = 40; SP_CBW = 52; SP_CBB = 176; SP_CNG = 180; SP_CNB = 184; SP_BF = 188; SP_N = 204
C_ID = 0; C_ONE = 128; C_L = 256; C_S127 = 384; C_S7 = 512; C_IOTA = 640; C_MASKN = 641; C_SUF = 769; C_BM = 897; C_N = 913


class Buf:
    __slots__ = ("name", "lw", "rd", "dsem", "dcount", "psum")

    def __init__(self, name, psum=False):
        self.name = name; self.lw = None; self.rd = []; self.dsem = None; self.dcount = 0; self.psum = psum


class Prog:
    def __init__(self, nc, stack):
        self.nc = nc; self.stack = stack
        self.ops = {e: [] for e in ENGS}
        self.esem = {}; self.ecount = {e: 0 for e in ENGS}
        self.waited = {e: {} for e in ENGS}; self.sems = {}
        for e in ENGS:
            s = stack.enter_context(nc.semaphore("es_" + e))
            self.esem[e] = s; self.sems["E" + e] = s
        self.nsem = 0; self.out_events = []

    def new_dsem(self, buf):
        if buf.dsem is None:
            self.nsem += 1
            key = "D%d" % self.nsem
            self.sems[key] = self.stack.enter_context(self.nc.semaphore("ds%d" % self.nsem))
            buf.dsem = key
        return buf.dsem

    def _deps(self, eng, reads, writes):
        ev = {}

        def add(e):
            if e is None:
                return
            k, v, en = e
            if en == "tensor" and eng == "tensor":
                return
            if ev.get(k, 0) < v:
                ev[k] = v
        for b in reads:
            add(b.lw)
            if b.psum:
                for r in b.rd:
                    if r[2] != eng:
                        add(r)
        for b in writes:
            add(b.lw)
            for r in b.rd:
                add(r)
        w = self.waited[eng]; out = []
        for k, v in ev.items():
            if w.get(k, 0) < v:
                w[k] = v; out.append((self.sems[k], v))
        return out

    def op(self, eng, fn, reads=(), writes=()):
        waits = self._deps(eng, reads, writes)
        self.ecount[eng] += 1
        n = self.ecount[eng]; sem = self.esem[eng]

        def run(e, waits=waits, fn=fn, sem=sem):
            for s, v in waits:
                e.wait_ge(s, v)
            getattr(e, fn[0])(*fn[1], **fn[2]).then_inc(sem, 1)
        self.ops[eng].append(run)
        evt = ("E" + eng, n, eng)
        for b in reads:
            b.rd.append(evt)
            if len(b.rd) > 64:
                b.rd = _compact(b.rd)
        for b in writes:
            b.lw = evt; b.rd = []
        return evt

    def dma(self, q, fn, sbuf, load, reads=(), writes=(), is_output=False):
        if load:
            waits = self._deps(q, list(reads), [sbuf] + list(writes))
        else:
            waits = self._deps(q, [sbuf] + list(reads), list(writes))
        key = self.new_dsem(sbuf)
        sbuf.dcount += 16
        val = sbuf.dcount; sem = self.sems[key]

        def run(e, waits=waits, fn=fn, sem=sem):
            for s, v in waits:
                e.wait_ge(s, v)
            getattr(e, fn[0])(*fn[1], **fn[2]).then_inc(sem, 16)
        self.ops[q].append(run)
        evt = (key, val, "dma")
        rds = list(reads) + ([] if load else [sbuf])
        wrs = list(writes) + ([sbuf] if load else [])
        for b in rds:
            b.rd.append(evt)
        for b in wrs:
            b.lw = evt; b.rd = []
        if is_output:
            self.out_events.append(evt)
        return evt

    def finish(self):
        fin = {}
        for k, v, _ in self.out_events:
            fin[k] = max(fin.get(k, 0), v)
        finw = [(self.sems[k], v) for k, v in fin.items()]

        def fin_run(e):
            for s, v in finw:
                e.wait_ge(s, v)
        self.ops["sync"].append(fin_run)
        ops = self.ops
        with self.nc.Block() as block:
            @block.tensor
            def _(e):
                for f in ops["tensor"]:
                    f(e)

            @block.vector
            def _(e):
                for f in ops["vector"]:
                    f(e)

            @block.scalar
            def _(e):
                for f in ops["scalar"]:
                    f(e)

            @block.gpsimd
            def _(e):
                for f in ops["gpsimd"]:
                    f(e)

            @block.sync
            def _(e):
                for f in ops["sync"]:
                    f(e)


def _I(name, *a, **kw):
    return (name, a, kw)


def _compact(evs):
    m = {}
    for k, v, en in evs:
        if k not in m or m[k][1] < v:
            m[k] = (k, v, en)
    return list(m.values())


KDBG = int(os.environ.get("KDBG", "99"))


class _Stop(Exception):
    pass


def build_nc(passes=None, npool=NPOOL, stop=None):
    passes = passes or PASSES
    nc = bass.Bass("TRN2", target_bir_lowering=False)
    di = lambda n, s, dt=F32: nc.dram_tensor(n, s, dt, kind="ExternalInput").ap()
    do = lambda n, s, dt=F32: nc.dram_tensor(n, s, dt, kind="ExternalOutput").ap()
    xp = di("xp", [SEQ, D]); xs = di("xs", [128, D])
    cache_kv = di("cache_kv", [DEPTH * npool * 128, 1024])
    cache_f = di("cache_f", [DEPTH * npool * 128, 8])
    sca = di("sca", [DEPTH, NSEQ * 2, 256]); scb = di("scb", [DEPTH, NSEQ * 30, 256])
    ptb = di("ptb", [1, NSEQ * NPAGES], I32)
    smallp = di("smallp", [128, SP_N]); consts = di("consts", [128, C_N])
    w_in = di("w_in", [DEPTH, D, PT]); w_a = di("w_a", [DEPTH, 256, D]); w_b = di("w_b", [DEPTH, 256, D])
    w_c = di("w_c", [DEPTH, 512, D]); w_o = di("w_o", [DEPTH, D, D])
    w_g = di("w_g", [DEPTH, D, FFN]); w_u = di("w_u", [DEPTH, D, FFN]); w_d = di("w_d", [DEPTH, FFN, D])
    y_p = do("y_p", [SEQ, D]); y_s = do("y_s", [128, D])
    k_p = do("k_p", [DEPTH, SEQ, 512]); v_p = do("v_p", [DEPTH, SEQ, 512]); lf_p = do("lf_p", [DEPTH, SEQ, 8])
    ca_p = do("ca_p", [DEPTH, 2, 256]); cb_p = do("cb_p", [DEPTH, 30, 256])
    k_s = do("k_s", [DEPTH, 128, 512]); v_s = do("v_s", [DEPTH, 128, 512]); lf_s = do("lf_s", [DEPTH, 128, 8])
    ca_s = do("ca_s", [DEPTH, NSEQ * 2, 256]); cb_s = do("cb_s", [DEPTH, NSEQ, 30, 256])
    kt_scr = nc.dram_tensor("kt_scr", [DEPTH, 128, 4, SEQ], BF16).ap()
    vp_scr = nc.dram_tensor("vp_scr", [DEPTH, 128, 16, 520], BF16).ap()
    c_scr = nc.dram_tensor("c_scr", [DEPTH, 128, 16, 8], F32).ap()

    with ExitStack() as st:
        P = Prog(nc, st)
        sb = lambda name, shape, dt: st.enter_context(nc.sbuf_tensor(name, shape, dt))
        V = lambda fn, r=(), w=(): P.op("vector", fn, r, w)
        A = lambda fn, r=(), w=(): P.op("scalar", fn, r, w)
        T = lambda fn, r=(), w=(): P.op("tensor", fn, r, w)

        banks = [st.enter_context(nc.psum_tensor("pb%d" % i, [128, 512], F32)) for i in range(6)]
        bankB = [Buf("pb%d" % i, True) for i in range(6)]
        free = list(range(6))
        tbanks = [st.enter_context(nc.psum_tensor("tb%d" % i, [128, 1024], BF16)) for i in range(2)]
        tbankB = [Buf("tb%d" % i, True) for i in range(2)]
        tbi = [0]

        def tget():
            i = tbi[0] % 2; tbi[0] += 1
            return tbanks[i], tbankB[i]

        def ps_get():
            i = free.pop(0)
            return i

        def ps_rel(i):
            free.append(i)

        cst = sb("cst", [128, C_N], F32); cstB = Buf("cst")
        cstb = sb("cstb", [128, 512], BF16); cstbB = Buf("cstb")
        smp = sb("smp", [128, SP_N], F32); smpB = Buf("smp")
        xT = sb("xT", [128, 8, 512], F32); xTB = [Buf("xT%d" % i) for i in range(8)]
        hT = sb("hT", [128, 8, 512], BF16); hTB = Buf("hT")
        QT = sb("QT", [128, 4, 512], BF16); QTB = Buf("QT")
        KVR = sb("KVR", [128, 8192 + 8320], BF16)
        KT = KVR[:, 0:8192].rearrange("p (j n) -> p j n", j=4)
        Vp = KVR[:, 8192:8192 + 8320].rearrange("p (b h e) -> p b h e", b=16, h=8)
        KTB = [Buf("KT%d" % i) for i in range(16)]; VpB = [Buf("Vp%d" % i) for i in range(16)]
        ctok = sb("ctok", [128, 16, 8], F32); ctokB = Buf("ctok")
        totA = sb("tot", [128, DEPTH, 8], F32); totBs = [Buf("tot%d" % i) for i in range(DEPTH)]
        OT = sb("OT", [65, 8, 512], BF16); OTB = Buf("OT")
        ua = sb("ua", [128, 2, 2 + 512], F32); uaB = Buf("ua")
        ub = sb("ub", [128, 2, 30 + 512], F32); ubB = Buf("ub")
        uaS = ua[:, :, 0:160].rearrange("p c (b t) -> p c b t", t=10)
        ubS = sb("ubS", [128, 2, 16, 38], F32)
        sbT = sb("sbT", [128, 2, 512], F32); sbTB = Buf("sbT")
        tmpA = sb("tmpA", [128, 2, 512], F32); tmpAB = Buf("tmpA")
        zb = sb("zb", [128, 2, 512], F32); zbB = Buf("zb")
        AinT = sb("AinT", [128, 2, 512], BF16); AinB = Buf("Ain")
        BinT = sb("BinT", [128, 2, 512], BF16); BinB = Buf("Bin")
        histA = sb("histA", [128, DEPTH, 2, 2], F32); histB_ = sb("histB", [128, DEPTH, 2, 30], F32)
        histAB = [Buf("hA%d" % l) for l in range(DEPTH)]; histBB = [Buf("hB%d" % l) for l in range(DEPTH)]
        f1 = [sb("f1_%d" % i, [128, 512], F32) for i in range(4)]; f1B = [Buf("f1_%d" % i) for i in range(4)]
        f1i = [0]
        pTt = [sb("pT%d" % i, [128, 512], BF16) for i in range(3)]; pTB = [Buf("pT%d" % i) for i in range(3)]
        stg = [sb("stg%d" % i, [128, 512], F32) for i in range(4)]; stgB = [Buf("stg%d" % i) for i in range(4)]
        stgi = [0]
        kbf = sb("kbf", [128, 512], BF16); kbfB = Buf("kbf")
        lft = sb("lft", [128, 8], F32); lftB = Buf("lft")
        sm8 = [sb("sm8_%d" % i, [128, 8], F32) for i in range(3)]; sm8B = [Buf("sm8_%d" % i) for i in range(3)]
        biasT = sb("biasT", [128, 16, 8], F32); biasTB = Buf("biasT")
        rl = sb("rl", [65, 512], F32); rlB = Buf("rl")
        rstdT = sb("rstdT", [128, 512], F32); rstdB = Buf("rstd")
        mrgT = sb("mrgT", [128, 8, 512], BF16); mrgB = Buf("mrg")
        aT = KVR[:, 0:HC * 512].rearrange("p (k n) -> p k n", k=HC); aTB = [Buf("aT%d" % i) for i in range(HC)]
        NSLOT = 3
        wsl = [sb("wsl%d" % i, [128, 4096], BF16) for i in range(NSLOT)]; wslB = [Buf("wsl%d" % i) for i in range(NSLOT)]
        wi = [0]
        idxf = sb("idxf", [128, 256], F32); idxi = sb("idxi", [128, 512], mybir.dt.uint32); idxB = Buf("idx")
        KVt = [KVR[:, 0:4096].rearrange("p (a n) -> p a n", a=4), KVR[:, 8192:12288].rearrange("p (a n) -> p a n", a=4)]
        Kt = [KVt[i][:, :, 0:512] for i in range(2)]; KtB = [Buf("Kt%d" % i) for i in range(2)]
        Vt = [KVt[i][:, :, 512:1024] for i in range(2)]; VtB = [Buf("Vt%d" % i) for i in range(2)]
        KTs = [KVR[:, 4096 + i * 2048:4096 + (i + 1) * 2048].rearrange("p (a n) -> p a n", a=4) for i in range(2)]; KTsB = [Buf("KTs%d" % i) for i in range(2)]
        lfS = sb("lfS", [128, 16, 8], F32); lfSB = Buf("lfS")
        lfn = sb("lfn", [128, 8], F32); lfnB = Buf("lfn")
        lfnew = sb("lfnew", [128, 8], F32); lfnewB = Buf("lfnew")
        NTs = sb("NTs", [128, 136], F32); NTsB = Buf("NTs")
        lfblk = sb("lfblk", [128, 128], F32); lfblkB = Buf("lfblk")
        pTn = sb("pTn", [128, 64], BF16); pTnB = Buf("pTn")
        Qbd = sb("Qbd", [128, 4, 16, 16], BF16); QbdB = Buf("Qbd")
        totS = sb("totS", [128, 16, 8], F32); totSB = Buf("totS")
        bpast = sb("bpast", [128, 16, 8], F32); bpastB = Buf("bpast")
        bnew = sb("bnew", [128, 8], F32); bnewB = Buf("bnew")
        spS = sb("spS", [128, 256], F32); spSB = Buf("spS")
        pTs = sb("pTs", [128, 256], BF16); pTsB = Buf("pTs")
        Oacc = sb("Oacc", [64, 128], F32); OaccB = Buf("Oacc")
        KTn = KVR[:, 12352:12864].rearrange("p (a n) -> p a n", a=4); KTnB = Buf("KTn")
        Vnew = KVR[:, 12864:13376]; VnewB = Buf("Vnew")
        Vn = KVR[:, 13376:13888]; VnB = Buf("Vn")
        stS = sb("stS", [128, 256], F32); stSB = Buf("stS")

        identf = cst[:, C_ID:C_ID + 128]; onesf = cst[:, C_ONE:C_ONE + 128]; Lf = cst[:, C_L:C_L + 128]
        s127 = cst[:, C_S127:C_S127 + 128]; s7 = cst[:, C_S7:C_S7 + 128]
        identb = cstb[:, 0:128]; onesb = cstb[:, 128:256]; trib = cstb[:, 256:384]; maskNb = cstb[:, 384:512]
        sufmat = cst[:, C_SUF:C_SUF + 128]; bmask = cst[:, C_BM:C_BM + 16]

        P.dma("sync", _I("dma_start", out=cst[:], in_=consts), cstB, True)
        P.dma("sync", _I("dma_start", out=smp[:], in_=smallp), smpB, True)
        V(_I("tensor_copy", out=cstb[:, 0:384], in_=cst[:, 0:384]), [cstB], [cstbB])
        V(_I("tensor_copy", out=cstb[:, 384:512], in_=cst[:, C_MASKN:C_MASKN + 128]), [cstB], [cstbB])
        P.dma("sync", _I("dma_start", out=idxi[:, 0:256].bitcast(I32), in_=ptb.to_broadcast([128, 256])), idxB, True)
        V(_I("tensor_copy", out=idxf[:], in_=idxi[:, 0:256].bitcast(I32)), [idxB], [idxB])
        V(_I("tensor_scalar", out=idxf[:], in0=idxf[:], scalar1=128.0, scalar2=cst[:, C_IOTA:C_IOTA + 1],
                                    op0=ALU.mult, op1=ALU.add), [idxB, cstB], [idxB])
        V(_I("tensor_copy", out=idxi[:, 0:256], in_=idxf[:]), [idxB], [idxB])
        V(_I("tensor_scalar", out=idxf[:], in0=idxf[:], scalar1=float(npool * 128), scalar2=None, op0=ALU.add), [idxB], [idxB])
        V(_I("tensor_copy", out=idxi[:, 256:512], in_=idxf[:]), [idxB], [idxB])
        for b_ in VpB:
            pass
        sampB = KtB + VtB + KTsB + [KTnB, VnewB, VnB]

        def handoff(src, dst):
            evs = []
            for b_ in src:
                if b_.lw is not None:
                    evs.append(b_.lw)
                evs.extend(b_.rd)
            evs = _compact(evs)
            for d_ in dst:
                d_.rd = _compact(d_.rd + evs)
        V(_I("memset", lfn[:], 0.0), [], [lfnB])
        V(_I("memset", OT[:], 0.0), [], [OTB])

        def f1get():
            i = f1i[0] % 4; f1i[0] += 1
            return f1[i], f1B[i]

        def stget():
            i = stgi[0] % 4; stgi[0] += 1
            return stg[i], stgB[i]

        def wload(parts):
            i = wi[0] % NSLOT; wi[0] += 1
            for (c0, a, b, src) in parts:
                dst = wsl[i][:, c0:c0 + a * b].rearrange("p (a b) -> p a b", a=a)
                P.dma("gpsimd", _I("dma_start", out=dst, in_=src), wslB[i], True)
            return wsl[i], wslB[i]

        def win_cols(l, c0, w):
            return w_in[l][:, c0:c0 + w].rearrange("(kc p) n -> p kc n", p=128)

        def norm(l_g_off, N, out_bf):
            for kc in range(8):
                A(_I("activation", out=hT[:, kc, 0:N], in_=xT[:, kc, 0:N], func=AF.Square), [xTB[kc]], [hTB])
            bi = ps_get()
            for kc in range(8):
                T(_I("matmul", banks[bi][:, 0:N], lhsT=onesb, rhs=hT[:, kc, 0:N], start=(kc == 0), stop=(kc == 7)),
                  [hTB, cstbB], [bankB[bi]])
            r, rB = rstdT, rstdB
            V(_I("tensor_scalar", out=r[:, 0:N], in0=banks[bi][:, 0:N], scalar1=1.0 / D, scalar2=EPS, op0=ALU.mult, op1=ALU.add),
              [bankB[bi]], [rB])
            ps_rel(bi)
            A(_I("activation", out=r[:, 0:N], in_=r[:, 0:N], func=AF.Ln), [rB], [rB])
            A(_I("activation", out=r[:, 0:N], in_=r[:, 0:N], func=AF.Exp, scale=-0.5), [rB], [rB])
            return r, rB

        def mark(name):
            if stop == name:
                raise _Stop()
        try:
          for (g0, N, is_s) in passes:
              nt = N // 128
              xsrc = xs if is_s else xp[g0:g0 + N, :]
              for tt in range(nt):
                  for half in range(2):
                      s_, sB_ = stget()
                      P.dma("sync", _I("dma_start", out=s_[:], in_=xsrc[tt * 128:(tt + 1) * 128, half * 512:(half + 1) * 512]), sB_, True)
                      bi = ps_get()
                      for c in range(4):
                          T(_I("transpose", out=banks[bi][:, c * 128:(c + 1) * 128], in_=s_[:, c * 128:(c + 1) * 128], identity=identf),
                            [sB_, cstB], [bankB[bi]])
                      for c in range(4):
                          kc = half * 4 + c
                          V(_I("tensor_copy", out=xT[:, kc, tt * 128:(tt + 1) * 128], in_=banks[bi][:, c * 128:(c + 1) * 128]),
                            [bankB[bi]], [xTB[kc]])
                      ps_rel(bi)

              for l in range(DEPTH):
                  g1 = lambda kc: smp[:, SP_G1 + l * 8 + kc:SP_G1 + l * 8 + kc + 1]
                  g2 = lambda kc: smp[:, SP_G2 + l * 8 + kc:SP_G2 + l * 8 + kc + 1]
                  mark("load")
                  r, rB = norm(None, N, True)
                  for kc in range(8):
                      V(_I("scalar_tensor_tensor", out=hT[:, kc, 0:N], in0=xT[:, kc, 0:N], scalar=g1(kc), in1=r[:, 0:N],
                                                                op0=ALU.mult, op1=ALU.mult), [xTB[kc], rB, smpB], [hTB])

                  def proj_fm(wt, wB, k0, nk, kcols, act_fn, rhs_fn=None, rdB=None):
                      bi = ps_get()
                      for k in range(nk):
                          T(_I("matmul", banks[bi][:, 0:N], lhsT=kcols(k), rhs=(rhs_fn(k) if rhs_fn else hT[:, k, 0:N]),
                                                    start=(k == 0), stop=(k == nk - 1)), [wB, rdB or hTB], [bankB[bi]])
                      return bi

                  mark("N1")
                  uaw = (lambda cc: uaS[:, cc, :, 2:10]) if is_s else (lambda cc: ua[:, cc, 2:2 + N])
                  ubw = (lambda cc: ubS[:, cc, :, 30:38]) if is_s else (lambda cc: ub[:, cc, 30:30 + N])
                  v3 = (lambda ap: ap.rearrange("p (b t) -> p b t", t=8)) if is_s else (lambda ap: ap)
                  if is_s:
                      for (src, nrow, dstf) in ((sca, 2, lambda cc, gsz, g: uaS[:, cc, :, 0:2]),):
                          s_, sB_ = stget()
                          P.dma("sync", _I("dma_start", out=s_[0:32, 0:256], in_=sca[l]), sB_, True)
                          for cc in range(2):
                              bi = ps_get()
                              T(_I("transpose", out=banks[bi][:, 0:32], in_=s_[0:32, cc * 128:(cc + 1) * 128], identity=identf[0:32, 0:32]),
                                [sB_, cstB], [bankB[bi]])
                              V(_I("tensor_copy", out=uaS[:, cc, :, 0:2], in_=banks[bi][:, 0:32].rearrange("p (b t) -> p b t", t=2)),
                                [bankB[bi]], [uaB])
                              ps_rel(bi)
                      for g in range(4):
                          s_, sB_ = stget()
                          P.dma("sync", _I("dma_start", out=s_[0:120, 0:256], in_=scb[l][g * 120:(g + 1) * 120, :]), sB_, True)
                          for cc in range(2):
                              bi = ps_get()
                              T(_I("transpose", out=banks[bi][:, 0:120], in_=s_[0:120, cc * 128:(cc + 1) * 128], identity=identf[0:120, 0:120]),
                                [sB_, cstB], [bankB[bi]])
                              V(_I("tensor_copy", out=ubS[:, cc, g * 4:(g + 1) * 4, 0:30], in_=banks[bi][:, 0:120].rearrange("p (b t) -> p b t", t=30)),
                                [bankB[bi]], [ubB])
                              ps_rel(bi)
                  elif g0 == 0:
                      V(_I("memset", ua[:, :, 0:2], 0.0), [], [uaB])
                      V(_I("memset", ub[:, :, 0:30], 0.0), [], [ubB])
                  else:
                      V(_I("tensor_copy", out=ua[:, :, 0:2], in_=histA[:, l, :, :]), [histAB[l]], [uaB])
                      V(_I("tensor_copy", out=ub[:, :, 0:30], in_=histB_[:, l, :, :]), [histBB[l]], [ubB])

                  for (c0, w) in ((OFF_SB, 512), (OFF_SH, 512), (OFF_GB, 256)):
                      wt, wB = wload([(0, 8, w, win_cols(l, c0, w))])
                      wv = wt[:, 0:8 * w].rearrange("p (k n) -> p k n", k=8)
                      for oc in range(w // 128):
                          col = c0 + oc * 128
                          bi = proj_fm(wt, wB, 0, 8, lambda k, oc=oc, wv=wv: wv[:, k, oc * 128:(oc + 1) * 128], None)
                          pv = banks[bi][:, 0:N]
                          if col < OFF_SC:
                              cc = (col - OFF_SB) // 128
                              A(_I("activation", out=sbT[:, cc, 0:N], in_=pv, func=AF.Identity), [bankB[bi]], [sbTB])
                          elif col < OFF_SH:
                              cc = (col - OFF_SC) // 128
                              A(_I("activation", out=tmpA[:, cc, 0:N], in_=pv, func=AF.Identity), [bankB[bi]], [tmpAB])
                          elif col < OFF_GA:
                              cc = (col - OFF_SH) // 128
                              V(_I("tensor_tensor", out=uaw(cc), in0=v3(pv), in1=v3(tmpA[:, cc, 0:N]), op=ALU.mult),
                                [bankB[bi], tmpAB], [uaB])
                          elif col < OFF_GB:
                              cc = (col - OFF_GA) // 128
                              A(_I("activation", out=tmpA[:, cc, 0:N], in_=pv, func=AF.Identity), [bankB[bi]], [tmpAB])
                          else:
                              cc = (col - OFF_GB) // 128
                              t_, tB_ = f1get()
                              A(_I("activation", out=t_[:, 0:N], in_=pv, func=AF.Sigmoid), [bankB[bi]], [tB_])
                              V(_I("tensor_tensor", out=ubw(cc), in0=v3(t_[:, 0:N]), in1=v3(tmpA[:, cc, 0:N]), op=ALU.mult),
                                [tB_, tmpAB], [ubB])
                          ps_rel(bi)

                  mark("P1")
                  caw = lambda j, cc: smp[:, SP_CAW + l * 6 + j * 2 + cc:SP_CAW + l * 6 + j * 2 + cc + 1]
                  cbw = lambda j, cc: smp[:, SP_CBW + l * 62 + j * 2 + cc:SP_CBW + l * 62 + j * 2 + cc + 1]
                  spc = lambda off, cc: smp[:, off + l * 2 + cc:off + l * 2 + cc + 1]
                  uain = (lambda cc, j: uaS[:, cc, :, j:j + 8]) if is_s else (lambda cc, j: ua[:, cc, j:j + N])
                  ubin = (lambda cc, j: ubS[:, cc, :, j:j + 8]) if is_s else (lambda cc, j: ub[:, cc, j:j + N])
                  for cc in range(2):
                      za, zaB = f1get()
                      V(_I("tensor_scalar", out=v3(za[:, 0:N]), in0=uain(cc, 0), scalar1=caw(0, cc), scalar2=None, op0=ALU.mult),
                        [uaB, smpB], [zaB])
                      for j in (1, 2):
                          V(_I("scalar_tensor_tensor", out=v3(za[:, 0:N]), in0=uain(cc, j), scalar=caw(j, cc), in1=v3(za[:, 0:N]),
                                                                                op0=ALU.mult, op1=ALU.add), [uaB, smpB, zaB], [zaB])
                      V(_I("tensor_tensor", out=AinT[:, cc, 0:N], in0=za[:, 0:N], in1=sbT[:, cc, 0:N], op=ALU.mult),
                        [zaB, sbTB], [AinB])
                      V(_I("tensor_scalar", out=v3(zb[:, cc, 0:N]), in0=ubin(cc, 0), scalar1=cbw(0, cc), scalar2=spc(SP_CBB, cc),
                                                         op0=ALU.mult, op1=ALU.add), [ubB, smpB], [zbB])
                      for j in range(1, 31):
                          V(_I("scalar_tensor_tensor", out=v3(zb[:, cc, 0:N]), in0=ubin(cc, j), scalar=cbw(j, cc), in1=v3(zb[:, cc, 0:N]),
                                                                         op0=ALU.mult, op1=ALU.add), [ubB, smpB, zbB], [zbB])
                  mark("taps")
                  b1 = ps_get(); b2 = ps_get()
                  sq, sqB = f1get()
                  for cc in range(2):
                      T(_I("matmul", banks[b1][:, 0:N], lhsT=onesf, rhs=zb[:, cc, 0:N], start=(cc == 0), stop=(cc == 1)),
                        [zbB, cstB], [bankB[b1]])
                  for cc in range(2):
                      A(_I("activation", out=sq[:, 0:N], in_=zb[:, cc, 0:N], func=AF.Square), [zbB], [sqB])
                      T(_I("matmul", banks[b2][:, 0:N], lhsT=onesf, rhs=sq[:, 0:N], start=(cc == 0), stop=(cc == 1)),
                        [sqB, cstB], [bankB[b2]])
                  mark("lnmm")
                  mean, meanB = f1get(); rs, rsB = f1get()
                  V(_I("tensor_scalar", out=mean[:, 0:N], in0=banks[b1][:, 0:N], scalar1=1.0 / 256, scalar2=None, op0=ALU.mult), [bankB[b1]], [meanB])
                  V(_I("tensor_tensor", out=rs[:, 0:N], in0=mean[:, 0:N], in1=mean[:, 0:N], op=ALU.mult), [meanB], [rsB])
                  V(_I("scalar_tensor_tensor", out=rs[:, 0:N], in0=banks[b2][:, 0:N], scalar=1.0 / 256, in1=rs[:, 0:N], op0=ALU.mult, op1=ALU.subtract),
                    [bankB[b2], rsB], [rsB])
                  V(_I("tensor_scalar", out=rs[:, 0:N], in0=rs[:, 0:N], scalar1=EPS, scalar2=None, op0=ALU.add), [rsB], [rsB])
                  A(_I("activation", out=rs[:, 0:N], in_=rs[:, 0:N], func=AF.Ln), [rsB], [rsB])
                  A(_I("activation", out=rs[:, 0:N], in_=rs[:, 0:N], func=AF.Exp, scale=-0.5), [rsB], [rsB])
                  ps_rel(b1); ps_rel(b2)
                  mark("lnrs")
                  for cc in range(2):
                      V(_I("tensor_tensor", out=zb[:, cc, 0:N], in0=zb[:, cc, 0:N], in1=mean[:, 0:N], op=ALU.subtract), [zbB, meanB], [zbB])
                      V(_I("tensor_tensor", out=zb[:, cc, 0:N], in0=zb[:, cc, 0:N], in1=rs[:, 0:N], op=ALU.mult), [zbB, rsB], [zbB])
                      A(_I("activation", out=BinT[:, cc, 0:N], in_=zb[:, cc, 0:N], func=AF.Silu, scale=spc(SP_CNG, cc), bias=spc(SP_CNB, cc)),
                        [zbB, smpB], [BinB])
                  mark("silu")
                  if is_s:
                      for cc in range(2):
                          bi = ps_get()
                          c_, cB_ = f1get()
                          V(_I("tensor_copy", out=c_[:, 0:32].rearrange("p (b t) -> p b t", t=2), in_=uaS[:, cc, :, 8:10]), [uaB], [cB_])
                          T(_I("transpose", out=banks[bi][0:32, 0:128], in_=c_[:, 0:32], identity=identf), [cB_, cstB], [bankB[bi]])
                          s_, sB_ = stget()
                          V(_I("tensor_copy", out=s_[0:32, 0:128], in_=banks[bi][0:32, 0:128]), [bankB[bi]], [sB_])
                          ps_rel(bi)
                          P.dma("sync", _I("dma_start", out=ca_s[l][:, cc * 128:(cc + 1) * 128], in_=s_[0:32, 0:128]), sB_, False, is_output=True)
                          for g in range(4):
                              bi = ps_get()
                              c_, cB_ = f1get()
                              V(_I("tensor_copy", out=c_[:, 0:120].rearrange("p (b t) -> p b t", t=30), in_=ubS[:, cc, g * 4:(g + 1) * 4, 8:38]), [ubB], [cB_])
                              T(_I("transpose", out=banks[bi][0:120, 0:128], in_=c_[:, 0:120], identity=identf),
                                [cB_, cstB], [bankB[bi]])
                              s_, sB_ = stget()
                              V(_I("tensor_copy", out=s_[0:120, 0:128], in_=banks[bi][0:120, 0:128]), [bankB[bi]], [sB_])
                              ps_rel(bi)
                              P.dma("sync", _I("dma_start",
                                  out=cb_s[l][g * 4:(g + 1) * 4, :, cc * 128:(cc + 1) * 128].rearrange("b t c -> (b t) c"), in_=s_[0:120, 0:128]), sB_, False, is_output=True)
                  elif g0 + N < SEQ:
                      V(_I("tensor_copy", out=histA[:, l, :, :], in_=ua[:, :, N:N + 2]), [uaB], [histAB[l]])
                      V(_I("tensor_copy", out=histB_[:, l, :, :], in_=ub[:, :, N:N + 30]), [ubB], [histBB[l]])
                  else:
                      for cc in range(2):
                          for (buf_, bB_, nr, dst) in ((ua, uaB, 2, ca_p), (ub, ubB, 30, cb_p)):
                              bi = ps_get()
                              hh = 2 if nr == 2 else 30
                              T(_I("transpose", out=banks[bi][0:nr, 0:128], in_=buf_[:, cc, hh + N - nr:hh + N], identity=identf),
                                [bB_, cstB], [bankB[bi]])
                              s_, sB_ = stget()
                              V(_I("tensor_copy", out=s_[0:nr, 0:128], in_=banks[bi][0:nr, 0:128]), [bankB[bi]], [sB_])
                              ps_rel(bi)
                              P.dma("sync", _I("dma_start", out=dst[l][:, cc * 128:(cc + 1) * 128], in_=s_[0:nr, 0:128]),
                                    sB_, False, is_output=True)

                  mark("conv")
                  gb0 = g0 // 128
                  kvset = sampB if is_s else (KTB + VpB)
                  handoff(aTB + KTB + VpB + sampB, kvset)
                  if not is_s:
                      V(_I("memset", Vp[:, :, :, 64:65], 1.0), [], VpB)
                  if (not is_s) and g0 > 0:
                      P.dma("sync", _I("dma_start", out=KT[:, :, 0:g0], in_=kt_scr[l][:, :, 0:g0]), KTB[0], True, writes=KTB[1:gb0])
                      P.dma("sync", _I("dma_start", out=KVR[:, 8192:8192 + gb0 * 520], in_=vp_scr[l][:, 0:gb0, :].rearrange("p b e -> p (b e)")),
                            VpB[0], True, writes=VpB[1:gb0])
                      P.dma("sync", _I("dma_start", out=ctok[:, 0:gb0, :], in_=c_scr[l][:, 0:gb0, :]), ctokB, True)
                  if g0 == 0 and not is_s:
                      V(_I("memset", totA[:, l, :], 0.0), [], [totBs[l]])
                  kout = k_s if is_s else k_p; vout = v_s if is_s else v_p; lfout = lf_s if is_s else lf_p
                  for (c0, w, kind) in ((OFF_K, 512, "k"), (OFF_V, 512, "v"), (OFF_F, 8, "f")):
                      mark("pre_" + kind)
                      wt, wB = wload([(0, 8, w, win_cols(l, c0, w))])
                      wv = wt[:, 0:8 * w].rearrange("p (k n) -> p k n", k=8)
                      for tt in range(nt):
                          gb = gb0 + tt
                          bi = ps_get()
                          for k in range(8):
                              T(_I("matmul", banks[bi][:, 0:w], lhsT=hT[:, k, tt * 128:(tt + 1) * 128], rhs=wv[:, k, :],
                                                                      start=(k == 0), stop=(k == 7)), [wB, hTB], [bankB[bi]])
                          rows = slice(g0 + tt * 128, g0 + (tt + 1) * 128)
                          if kind in ("k", "v"):
                              s_, sB_ = stget()
                              A(_I("activation", out=s_[:], in_=banks[bi][:], func=AF.Identity), [bankB[bi]], [sB_])
                              dst = kout if kind == "k" else vout
                              if KDBG >= 2: P.dma("sync", _I("dma_start", out=dst[l][rows, :], in_=s_[:]), sB_, False, is_output=True)
                          if kind == "k":
                              if KDBG >= 3: V(_I("tensor_copy", out=kbf[:], in_=s_[:]), [sB_], [kbfB])
                              ps_rel(bi)
                              tb_, tbB_ = tget()
                              pbf = tb_[:, 0:512]
                              for j in range(4):
                                  if KDBG >= 4: T(_I("transpose", out=pbf[:, j * 128:(j + 1) * 128], in_=kbf[:, j * 128:(j + 1) * 128], identity=identb),
                                    [kbfB, cstbB], [tbB_])
                              if is_s:
                                  V(_I("tensor_copy", out=KTn[:], in_=pbf.rearrange("p (j n) -> p j n", j=4)), [tbB_], [KTnB])
                              elif KDBG >= 5:
                                  V(_I("tensor_copy", out=KT[:, :, gb * 128:(gb + 1) * 128], in_=pbf.rearrange("p (j n) -> p j n", j=4)),
                                    [tbB_], [KTB[gb]])
                          elif kind == "v":
                              if is_s:
                                  V(_I("tensor_copy", out=Vnew, in_=s_[:]), [sB_], [VnewB])
                              else:
                                  V(_I("tensor_copy", out=Vp[:, gb, :, 0:64], in_=s_[:].rearrange("p (h d) -> p h d", h=8)),
                                    [sB_], [VpB[gb]])
                              ps_rel(bi)
                          else:
                              bfb = smp[:, SP_BF + l * 8:SP_BF + l * 8 + 8]
                              lt = lfnew if is_s else lft; ltB = lfnewB if is_s else lftB
                              V(_I("tensor_tensor", out=sm8[0][:], in0=banks[bi][:, 0:8], in1=bfb, op=ALU.add), [bankB[bi], smpB], [sm8B[0]])
                              ps_rel(bi)
                              A(_I("activation", out=sm8[0][:], in_=sm8[0][:], func=AF.Exp, scale=-1.0), [sm8B[0]], [sm8B[0]])
                              A(_I("activation", out=sm8[0][:], in_=sm8[0][:], func=AF.Ln, bias=cst[:, C_ONE:C_ONE + 1], scale=1.0), [sm8B[0], cstB], [sm8B[0]])
                              V(_I("tensor_scalar", out=lt[:], in0=sm8[0][:], scalar1=-1.0, scalar2=None, op0=ALU.mult), [sm8B[0]], [ltB])
                              P.dma("sync", _I("dma_start", out=lfout[l][rows, :], in_=lt[:]), ltB, False, is_output=True)
                              if not is_s:
                                  b1 = ps_get()
                                  T(_I("matmul", banks[b1][:, 0:8], lhsT=Lf, rhs=lft[:], start=True, stop=True), [lftB, cstB], [bankB[b1]])
                                  T(_I("matmul", banks[b1][:, 8:16], lhsT=onesf, rhs=lft[:], start=True, stop=True), [lftB, cstB], [bankB[b1]])
                                  V(_I("tensor_tensor", out=ctok[:, gb, :], in0=banks[b1][:, 0:8], in1=totA[:, l, :], op=ALU.add), [bankB[b1], totBs[l]], [ctokB])
                                  V(_I("tensor_tensor", out=totA[:, l, :], in0=banks[b1][:, 8:16], in1=totA[:, l, :], op=ALU.add), [bankB[b1], totBs[l]], [totBs[l]])
                                  ps_rel(b1)
                  mark("kvf")
                  if (not is_s) and g0 + N < SEQ:
                      P.dma("sync", _I("dma_start", out=kt_scr[l][:, :, g0:g0 + N], in_=KT[:, :, g0:g0 + N]), KTB[gb0], False, reads=KTB[gb0 + 1:gb0 + nt])
                      P.dma("sync", _I("dma_start", out=vp_scr[l][:, gb0:gb0 + nt, :].rearrange("p b e -> p (b e)"), in_=KVR[:, 8192 + gb0 * 520:8192 + (gb0 + nt) * 520]),
                            VpB[gb0], False, reads=VpB[gb0 + 1:gb0 + nt])
                      P.dma("sync", _I("dma_start", out=c_scr[l][:, gb0:gb0 + nt, :], in_=ctok[:, gb0:gb0 + nt, :]), ctokB, False)

                  mark("P2")
                  wt, wB = wload([(0, 8, 512, win_cols(l, OFF_Q, 512))])
                  wv = wt[:, 0:4096].rearrange("p (k n) -> p k n", k=8)
                  for j in range(4):
                      bi = proj_fm(wt, wB, 0, 8, lambda k, j=j, wv=wv: wv[:, k, j * 128:(j + 1) * 128], None)
                      A(_I("activation", out=QT[:, j, 0:N], in_=banks[bi][:, 0:N], func=AF.Identity, scale=0.125), [bankB[bi]], [QTB])
                      ps_rel(bi)

                  mark("P3")
                  if not is_s:
                      nkb = (g0 + N) // 128
                      bc_ = ps_get()
                      T(_I("matmul", banks[bc_][:, 0:8], lhsT=s127, rhs=ctok[:, nkb - 1, :], start=True, stop=True), [ctokB, cstB], [bankB[bc_]])
                      V(_I("tensor_copy", out=sm8[1][:], in_=banks[bc_][:, 0:8]), [bankB[bc_]], [sm8B[1]])
                      ps_rel(bc_)
                      V(_I("tensor_tensor", out=biasT[:, 0:nkb, :], in0=sm8[1][:].unsqueeze(1).to_broadcast([128, nkb, 8]), in1=ctok[:, 0:nkb, :], op=ALU.subtract),
                        [sm8B[1], ctokB], [biasTB])
                      for h in range(NH):
                          j = h // 2; r0 = 64 * (h % 2)
                          bo = ps_get()
                          pend = None

                          def s_mm(kb):
                              cl = max(0, 128 * kb - g0)
                              bs = ps_get()
                              T(_I("matmul", banks[bs][:, cl:N], lhsT=KT[r0:r0 + 64, j, kb * 128:(kb + 1) * 128], rhs=QT[r0:r0 + 64, j, cl:N], start=True, stop=True),
                                [KTB[kb], QTB], [bankB[bs]])
                              return bs, cl
                          cur = s_mm(0)
                          for kb in range(nkb):
                              nxt = s_mm(kb + 1) if kb + 1 < nkb else None
                              bs, cl = cur
                              pi = (h * 16 + kb) % 3
                              A(_I("activation", out=pTt[pi][:, cl:N], in_=banks[bs][:, cl:N], func=AF.Exp, bias=biasT[:, kb, h:h + 1], scale=1.0),
                                [bankB[bs], biasTB], [pTB[pi]])
                              ps_rel(bs)
                              if 128 * kb >= g0:
                                  V(_I("tensor_tensor", out=pTt[pi][:, cl:cl + 128], in0=pTt[pi][:, cl:cl + 128], in1=trib, op=ALU.mult),
                                    [pTB[pi], cstbB], [pTB[pi]])
                              T(_I("matmul", banks[bo][0:65, cl:N], lhsT=Vp[:, kb, h, :], rhs=pTt[pi][:, cl:N], start=(kb == 0), stop=(kb == nkb - 1)),
                                [VpB[kb], pTB[pi]], [bankB[bo]])
                              cur = nxt
                          V(_I("reciprocal", out=rl[64:65, 0:N], in_=banks[bo][64:65, 0:N]), [bankB[bo]], [rlB])
                          bb = ps_get()
                          T(_I("matmul", banks[bb][0:64, 0:N], lhsT=onesf[64:65, 0:64], rhs=rl[64:65, 0:N], start=True, stop=True), [rlB, cstB], [bankB[bb]])
                          t_, tB_ = f1get()
                          A(_I("activation", out=t_[0:64, 0:N], in_=banks[bb][0:64, 0:N], func=AF.Identity), [bankB[bb]], [tB_])
                          ps_rel(bb)
                          V(_I("tensor_tensor", out=OT[0:64, h, 0:N], in0=banks[bo][0:64, 0:N], in1=t_[0:64, 0:N], op=ALU.mult), [bankB[bo], tB_], [OTB])
                          ps_rel(bo)
                  else:
                      V(_I("tensor_tensor", out=lfblk[:].rearrange("p (b h) -> p b h", b=16), in0=lfnew[:].unsqueeze(1).to_broadcast([128, 16, 8]),
                                      in1=bmask.unsqueeze(2).to_broadcast([128, 16, 8]), op=ALU.mult), [lfnewB, cstB], [lfblkB])
                      bx = ps_get()
                      T(_I("matmul", banks[bx][:, 0:128], lhsT=onesf, rhs=lfblk[:], start=True, stop=True), [lfblkB, cstB], [bankB[bx]])
                      T(_I("matmul", banks[bx][:, 128:136], lhsT=sufmat, rhs=lfnew[:], start=True, stop=True), [lfnewB, cstB], [bankB[bx]])
                      V(_I("tensor_copy", out=NTs[:], in_=banks[bx][:, 0:136]), [bankB[bx]], [NTsB])
                      ps_rel(bx)
                      V(_I("memset", Qbd[:], 0.0), [], [QbdB])
                      for j in range(4):
                          V(_I("tensor_copy", out=Qbd[0:64, j, :, 0:8], in_=QT[0:64, j, 0:128].rearrange("p (b t) -> p b t", t=8)), [QTB], [QbdB])
                          V(_I("tensor_copy", out=Qbd[64:128, j, :, 8:16], in_=QT[64:128, j, 0:128].rearrange("p (b t) -> p b t", t=8)), [QTB], [QbdB])
                      for b in range(NSEQ):
                          for pg in range(NPAGES):
                              P.dma("gpsimd", _I("indirect_dma_start",
                                  out=lfS[:, pg, :], out_offset=None, in_=cache_f,
                                  in_offset=bass.IndirectOffsetOnAxis(ap=idxi[:, l * 256 + b * 16 + pg:l * 256 + b * 16 + pg + 1], axis=0)), lfSB, True, reads=[idxB])
                          bw = ps_get(); bt = ps_get()
                          lfS2 = lfS[:].rearrange("p a h -> p (a h)")
                          T(_I("matmul", banks[bw][:, 0:128], lhsT=Lf, rhs=lfS2, start=True, stop=True), [lfSB, cstB], [bankB[bw]])
                          T(_I("matmul", banks[bt][:, 0:128], lhsT=onesf, rhs=lfS2, start=True, stop=True), [lfSB, cstB], [bankB[bt]])
                          V(_I("tensor_copy", out=totS[:].rearrange("p a h -> p (a h)"), in_=banks[bt][:, 0:128]), [bankB[bt]], [totSB])
                          ps_rel(bt)
                          V(_I("tensor_tensor", out=totS[:, 15, :], in0=totS[:, 15, :], in1=NTs[:, b * 8:(b + 1) * 8], op=ALU.add), [totSB, NTsB], [totSB])
                          for pg in range(14, -1, -1):
                              V(_I("tensor_tensor", out=totS[:, pg, :], in0=totS[:, pg, :], in1=totS[:, pg + 1, :], op=ALU.add), [totSB], [totSB])
                          V(_I("tensor_tensor", out=bpast[:].rearrange("p a h -> p (a h)"), in0=totS[:].rearrange("p a h -> p (a h)"), in1=banks[bw][:, 0:128], op=ALU.subtract),
                            [totSB, bankB[bw]], [bpastB])
                          ps_rel(bw)
                          mark("sa1")
                          for qd in range(5):
                              bo = ps_get()
                              if qd < 4:
                                  sl = (b * 4 + qd) % 2
                                  for pg in range(4):
                                      col = b * 16 + qd * 4 + pg
                                      P.dma("gpsimd", _I("indirect_dma_start",
                                          out=KVt[sl][:, pg, :], out_offset=None, in_=cache_kv,
                                          in_offset=bass.IndirectOffsetOnAxis(ap=idxi[:, l * 256 + col:l * 256 + col + 1], axis=0)), KtB[sl], True, reads=[idxB], writes=[VtB[sl]])
                                  for pp in range(2):
                                      tb_, tbB_ = tget()
                                      pbf = tb_[:].rearrange("p (j n) -> p j n", j=4)
                                      for p2 in range(2):
                                          pg = pp * 2 + p2
                                          for j in range(4):
                                              T(_I("transpose", out=pbf[:, j, p2 * 128:(p2 + 1) * 128], in_=Kt[sl][:, pg, j * 128:(j + 1) * 128], identity=identb),
                                                [KtB[sl], cstbB], [tbB_])
                                      V(_I("tensor_copy", out=KTs[sl][:, :, pp * 256:(pp + 1) * 256], in_=pbf), [tbB_], [KTsB[sl]])
                                  mark("sa2")
                                  bs = ps_get()
                                  for pg in range(4):
                                      for j in range(4):
                                          T(_I("matmul", banks[bs][:, pg * 64 + j * 16:pg * 64 + j * 16 + 16], lhsT=KTs[sl][:, j, pg * 128:(pg + 1) * 128],
                                               rhs=Qbd[:, j, b, :], start=True, stop=True), [KTsB[sl], QbdB], [bankB[bs]])
                                  V(_I("tensor_tensor", out=spS[:].rearrange("p (a h t) -> p a h t", a=4, h=8), in0=banks[bs][:, 0:256].rearrange("p (a h t) -> p a h t", a=4, h=8),
                                                                     in1=bpast[:, qd * 4:(qd + 1) * 4, :].unsqueeze(3).to_broadcast([128, 4, 8, 8]), op=ALU.add), [bankB[bs], bpastB], [spSB])
                                  ps_rel(bs)
                                  A(_I("activation", out=pTs[:], in_=spS[:], func=AF.Exp), [spSB], [pTsB])
                                  mark("sa3")
                                  for h in range(NH):
                                      for pg in range(4):
                                          T(_I("matmul", banks[bo][0:64, h * 8:(h + 1) * 8], lhsT=Vt[sl][:, pg, h * 64:(h + 1) * 64], rhs=pTs[:, pg * 64 + h * 8:pg * 64 + h * 8 + 8],
                                                                                  start=(pg == 0), stop=(pg == 3)), [VtB[sl], pTsB], [bankB[bo]])
                                  for pg in range(4):
                                      T(_I("matmul", banks[bo][0:64, 64:128], lhsT=onesb[:, 0:64], rhs=pTs[:, pg * 64:(pg + 1) * 64], start=(pg == 0), stop=(pg == 3)),
                                        [pTsB, cstbB], [bankB[bo]])
                              else:
                                  bs = ps_get()
                                  for j in range(4):
                                      T(_I("matmul", banks[bs][:, j * 16:(j + 1) * 16], lhsT=KTn[:, j, :], rhs=Qbd[:, j, b, :],
                                           start=True, stop=True), [KTnB, QbdB], [bankB[bs]])
                                  if KDBG >= 11: V(_I("tensor_tensor", out=spS[:, 0:64].rearrange("p (h t) -> p h t", h=8), in0=banks[bs][:, 0:64].rearrange("p (h t) -> p h t", h=8),
                                       in1=NTs[:, 128:136].unsqueeze(2).to_broadcast([128, 8, 8]), op=ALU.add), [bankB[bs], NTsB], [spSB])
                                  ps_rel(bs)
                                  if KDBG >= 12: A(_I("activation", out=pTn[:], in_=spS[:, 0:64], func=AF.Exp), [spSB], [pTnB])
                                  if KDBG >= 13: V(_I("tensor_tensor", out=pTn[:].rearrange("p (h t) -> p h t", h=8), in0=pTn[:].rearrange("p (h t) -> p h t", h=8),
                                       in1=maskNb[:, b * 8:(b + 1) * 8].unsqueeze(1).to_broadcast([128, 8, 8]), op=ALU.mult), [pTnB, cstbB], [pTnB])
                                  for h in range(NH):
                                      if KDBG >= 14: T(_I("matmul", banks[bo][0:64, h * 8:(h + 1) * 8], lhsT=Vnew[:, h * 64:(h + 1) * 64], rhs=pTn[:, h * 8:(h + 1) * 8], start=True, stop=True),
                                        [VnewB, pTnB], [bankB[bo]])
                                  if KDBG >= 15: T(_I("matmul", banks[bo][0:64, 64:128], lhsT=onesb[:, 0:64], rhs=pTn[:, 0:64], start=True, stop=True), [pTnB, cstbB], [bankB[bo]])
                              if qd == 0:
                                  V(_I("tensor_copy", out=Oacc[:], in_=banks[bo][0:64, 0:128]), [bankB[bo]], [OaccB])
                              else:
                                  V(_I("tensor_tensor", out=Oacc[:], in0=Oacc[:], in1=banks[bo][0:64, 0:128], op=ALU.add), [bankB[bo], OaccB], [OaccB])
                              ps_rel(bo)
                              mark("sa4")
                              if qd == 3:
                                  mark("sa4b")
                          mark("sa5")
                          V(_I("reciprocal", out=rl[0:64, 0:64], in_=Oacc[:, 64:128]), [OaccB], [rlB])
                          V(_I("tensor_tensor", out=OT[0:64, :, b * 8:(b + 1) * 8], in0=Oacc[:, 0:64].rearrange("p (h t) -> p h t", h=8),
                                                      in1=rl[0:64, 0:64].rearrange("p (h t) -> p h t", h=8), op=ALU.mult), [OaccB, rlB], [OTB])

                  mark("attn")
                  wcv = w_c[l].rearrange("(h d) n -> d h n", d=64)
                  for oc in range(8):
                      cs = slice(oc * 128, (oc + 1) * 128)
                      wg_, wgB = wload([(br * 1024, 8, 128, win_cols(l, OFF_GATE + br * 1024 + oc * 128, 128)) for br in range(3)])
                      wo_, woB = wload([(0, 2, 128, w_a[l][:, cs].rearrange("(c p) n -> p c n", p=128)),
                                        (256, 2, 128, w_b[l][:, cs].rearrange("(c p) n -> p c n", p=128))])
                      iS = (wi[0] - 1) % NSLOT
                      P.dma("gpsimd", _I("dma_start", out=wsl[iS][0:64, 512:1536].rearrange("p (h n) -> p h n", h=8), in_=wcv[:, :, cs]), wslB[iS], True)
                      sg = []
                      for br in range(3):
                          gv = wg_[:, br * 1024:(br + 1) * 1024].rearrange("p (k n) -> p k n", k=8)
                          bi = proj_fm(wg_, wgB, 0, 8, lambda k, gv=gv: gv[:, k, :], None)
                          t_, tB_ = f1get()
                          A(_I("activation", out=t_[:, 0:N], in_=banks[bi][:, 0:N], func=AF.Sigmoid), [bankB[bi]], [tB_])
                          ps_rel(bi)
                          sg.append((t_, tB_))
                      wav = wo_[:, 0:256].rearrange("p (c n) -> p c n", c=2); wbv = wo_[:, 256:512].rearrange("p (c n) -> p c n", c=2)
                      wcs = wo_[0:64, 512:1536].rearrange("p (h n) -> p h n", h=8)
                      bi = proj_fm(wo_, woB, 0, 2, lambda k: wav[:, k, :], None, rhs_fn=lambda k: AinT[:, k, 0:N], rdB=AinB)
                      V(_I("tensor_tensor", out=sg[0][0][:, 0:N], in0=banks[bi][:, 0:N], in1=sg[0][0][:, 0:N], op=ALU.mult), [bankB[bi], sg[0][1]], [sg[0][1]])
                      ps_rel(bi)
                      bi = proj_fm(wo_, woB, 0, 2, lambda k: wbv[:, k, :], None, rhs_fn=lambda k: BinT[:, k, 0:N], rdB=BinB)
                      V(_I("tensor_tensor", out=sg[1][0][:, 0:N], in0=banks[bi][:, 0:N], in1=sg[1][0][:, 0:N], op=ALU.mult), [bankB[bi], sg[1][1]], [sg[1][1]])
                      ps_rel(bi)
                      bi = proj_fm(wo_, woB, 0, 8, lambda k: wcs[:, k, :], None, rhs_fn=lambda k: OT[0:64, k, 0:N], rdB=OTB)
                      V(_I("tensor_tensor", out=sg[2][0][:, 0:N], in0=banks[bi][:, 0:N], in1=sg[2][0][:, 0:N], op=ALU.mult), [bankB[bi], sg[2][1]], [sg[2][1]])
                      ps_rel(bi)
                      V(_I("tensor_tensor", out=sg[0][0][:, 0:N], in0=sg[0][0][:, 0:N], in1=sg[1][0][:, 0:N], op=ALU.add), [sg[0][1], sg[1][1]], [sg[0][1]])
                      V(_I("tensor_tensor", out=mrgT[:, oc, 0:N], in0=sg[0][0][:, 0:N], in1=sg[2][0][:, 0:N], op=ALU.add), [sg[0][1], sg[2][1]], [mrgB])

                  mark("P4")
                  for og in range(2):
                      wt, wB = wload([(0, 8, 512, w_o[l][:, og * 512:(og + 1) * 512].rearrange("(kc p) n -> p kc n", p=128))])
                      wv = wt[:, 0:4096].rearrange("p (k n) -> p k n", k=8)
                      for o4 in range(4):
                          oc = og * 4 + o4
                          bi = proj_fm(wt, wB, 0, 8, lambda k, o4=o4, wv=wv: wv[:, k, o4 * 128:(o4 + 1) * 128], None, rhs_fn=lambda k: mrgT[:, k, 0:N], rdB=mrgB)
                          V(_I("tensor_tensor", out=xT[:, oc, 0:N], in0=xT[:, oc, 0:N], in1=banks[bi][:, 0:N], op=ALU.add), [bankB[bi], xTB[oc]], [xTB[oc]])
                          ps_rel(bi)

                  mark("Wo")
                  r, rB = norm(None, N, True)
                  for kc in range(8):
                      V(_I("scalar_tensor_tensor", out=hT[:, kc, 0:N], in0=xT[:, kc, 0:N], scalar=g2(kc), in1=r[:, 0:N],
                                                                op0=ALU.mult, op1=ALU.mult), [xTB[kc], rB, smpB], [hTB])
                  handoff(KTB + VpB + sampB, aTB)
                  for hg in range(11):
                      wg_, wgB = wload([(0, 8, 256, w_g[l][:, hg * 256:(hg + 1) * 256].rearrange("(kc p) n -> p kc n", p=128)),
                                        (2048, 8, 256, w_u[l][:, hg * 256:(hg + 1) * 256].rearrange("(kc p) n -> p kc n", p=128))])
                      gv = wg_[:, 0:2048].rearrange("p (k n) -> p k n", k=8); uv = wg_[:, 2048:4096].rearrange("p (k n) -> p k n", k=8)
                      for h2 in range(2):
                          hi = hg * 2 + h2
                          b1 = proj_fm(wg_, wgB, 0, 8, lambda k, h2=h2, gv=gv: gv[:, k, h2 * 128:(h2 + 1) * 128], None)
                          t_, tB_ = f1get()
                          A(_I("activation", out=t_[:, 0:N], in_=banks[b1][:, 0:N], func=AF.Silu), [bankB[b1]], [tB_])
                          ps_rel(b1)
                          b2 = proj_fm(wg_, wgB, 0, 8, lambda k, h2=h2, uv=uv: uv[:, k, h2 * 128:(h2 + 1) * 128], None)
                          V(_I("tensor_tensor", out=aT[:, hi, 0:N], in0=banks[b2][:, 0:N], in1=t_[:, 0:N], op=ALU.mult), [bankB[b2], tB_], [aTB[hi]])
                          ps_rel(b2)
                  for oc in range(8):
                      wt, wB = wload([(0, HC, 128, w_d[l][:, oc * 128:(oc + 1) * 128].rearrange("(hc p) n -> p hc n", p=128))])
                      wv = wt[:, 0:HC * 128].rearrange("p (k n) -> p k n", k=HC)
                      bi = ps_get()
                      for k in range(HC):
                          T(_I("matmul", banks[bi][:, 0:N], lhsT=wv[:, k, :], rhs=aT[:, k, 0:N], start=(k == 0), stop=(k == HC - 1)), [wB, aTB[k]], [bankB[bi]])
                      V(_I("tensor_tensor", out=xT[:, oc, 0:N], in0=xT[:, oc, 0:N], in1=banks[bi][:, 0:N], op=ALU.add), [bankB[bi], xTB[oc]], [xTB[oc]])
                      ps_rel(bi)

              r, rB = norm(None, N, True)
              yout = y_s if is_s else y_p
              for tt in range(nt):
                  for half in range(2):
                      bi = ps_get()
                      for c in range(4):
                          kc = half * 4 + c
                          t_, tB_ = f1get()
                          V(_I("scalar_tensor_tensor", out=t_[:, 0:128], in0=xT[:, kc, tt * 128:(tt + 1) * 128], scalar=smp[:, SP_GF + kc:SP_GF + kc + 1],
                                                                                  in1=r[:, tt * 128:(tt + 1) * 128], op0=ALU.mult, op1=ALU.mult), [xTB[kc], rB, smpB], [tB_])
                          T(_I("transpose", out=banks[bi][:, c * 128:(c + 1) * 128], in_=t_[:, 0:128], identity=identf), [tB_, cstB], [bankB[bi]])
                      s_, sB_ = stget()
                      A(_I("activation", out=s_[:], in_=banks[bi][:], func=AF.Identity), [bankB[bi]], [sB_])
                      ps_rel(bi)
                      rows = slice(g0 + tt * 128, g0 + (tt + 1) * 128) if not is_s else slice(tt * 128, (tt + 1) * 128)
                      P.dma("sync", _I("dma_start", out=yout[rows, half * 512:(half + 1) * 512], in_=s_[:]), sB_, False, is_output=True)

        except _Stop:
            pass
        P.finish()
    return nc


def _consts():
    c = np.zeros((128, C_N), np.float32)
    c[:, C_ID:C_ID + 128] = np.eye(128, dtype=np.float32)
    c[:, C_ONE:C_ONE + 128] = 1.0
    c[:, C_L:C_L + 128] = np.triu(np.ones((128, 128), np.float32))
    c[127, C_S127:C_S127 + 128] = 1.0
    c[7, C_S7:C_S7 + 128] = 1.0
    c[:, C_IOTA] = np.arange(128, dtype=np.float32)
    sidx = np.arange(128)
    for b in range(16):
        for t in range(8):
            c[:, C_MASKN + b * 8 + t] = ((sidx // 8 == b) & (sidx % 8 <= t)).astype(np.float32)
    c[:, C_SUF:C_SUF + 128] = ((sidx[:, None] // 8 == sidx[None, :] // 8) & (sidx[:, None] % 8 > sidx[None, :] % 8)).astype(np.float32)
    c[:, C_BM:C_BM + 16] = (sidx[:, None] // 8 == np.arange(16)[None, :]).astype(np.float32)
    return c


def kernel(x_prompt, x_sample, cache_k, cache_v, cache_logf, state_conv_a, state_conv_b, page_table,
           norm1_g, w_in, b_f, conv_a_w, conv_b_w, conv_b_bias, cf_norm_g, cf_norm_b,
           w_a_out, w_b_out, w_c_out, w_o, norm2_g, w_ffn_gate, w_ffn_up, w_ffn_down, final_norm_g):
    f = lambda a: np.ascontiguousarray(np.asarray(a))
    sp = np.zeros((128, SP_N), np.float32)
    sp[:, SP_G1:SP_G1 + 16] = np.asarray(norm1_g).reshape(2, 8, 128).transpose(2, 0, 1).reshape(128, 16)
    sp[:, SP_G2:SP_G2 + 16] = np.asarray(norm2_g).reshape(2, 8, 128).transpose(2, 0, 1).reshape(128, 16)
    sp[:, SP_GF:SP_GF + 8] = np.asarray(final_norm_g).reshape(8, 128).T
    sp[:, SP_CAW:SP_CAW + 12] = np.asarray(conv_a_w).reshape(2, 3, 2, 128).transpose(3, 0, 1, 2).reshape(128, 12)
    sp[:, SP_CBW:SP_CBW + 124] = np.asarray(conv_b_w).reshape(2, 31, 2, 128).transpose(3, 0, 1, 2).reshape(128, 124)
    sp[:, SP_CBB:SP_CBB + 4] = np.asarray(conv_b_bias).reshape(2, 2, 128).transpose(2, 0, 1).reshape(128, 4)
    sp[:, SP_CNG:SP_CNG + 4] = np.asarray(cf_norm_g).reshape(2, 2, 128).transpose(2, 0, 1).reshape(128, 4)
    sp[:, SP_CNB:SP_CNB + 4] = np.asarray(cf_norm_b).reshape(2, 2, 128).transpose(2, 0, 1).reshape(128, 4)
    sp[:, SP_BF:SP_BF + 16] = np.broadcast_to(np.asarray(b_f).reshape(1, 16), (128, 16))
    consts = _consts()
    ckv = np.concatenate([np.asarray(cache_k).reshape(DEPTH * NPOOL * 128, 512), np.asarray(cache_v).reshape(DEPTH * NPOOL * 128, 512)], axis=1)
    cf = f(cache_logf).reshape(DEPTH * NPOOL * 128, 8)
    shared = dict(cache_kv=ckv, cache_f=cf, smallp=sp, consts=consts, w_in=f(w_in), w_a=f(w_a_out), w_b=f(w_b_out),
                  w_c=f(w_c_out), w_o=f(w_o), w_g=f(w_ffn_gate), w_u=f(w_ffn_up), w_d=f(w_ffn_down))
    xpn = np.asarray(x_prompt); xsn = np.asarray(x_sample)
    sca = np.asarray(state_conv_a); scb = np.asarray(state_conv_b); pt = np.asarray(page_table)
    in_maps = []
    for c in range(NCORES):
        sq = slice(c * NSEQ, (c + 1) * NSEQ)
        m = dict(shared)
        m["xp"] = f(xpn[c]); m["xs"] = f(xsn[sq].reshape(128, D))
        m["sca"] = f(sca[:, sq].reshape(DEPTH, NSEQ * 2, 256)); m["scb"] = f(scb[:, sq].reshape(DEPTH, NSEQ * 30, 256))
        m["ptb"] = f(pt[sq].reshape(1, NSEQ * NPAGES).astype(np.int32))
        in_maps.append(m)
    nc = build_nc()
    res = run_bass_kernel_spmd(nc, in_maps, core_ids=list(range(NCORES)))
    R = res.results
    st = lambda k: np.stack([np.asarray(r[k]) for r in R])
    y_prompt = st("y_p").reshape(8, SEQ, D)
    y_sample = st("y_s").reshape(128, DSEQ, D)
    k_prompt = st("k_p").transpose(1, 0, 2, 3).reshape(DEPTH, 8, 16, 128, NH, DH)
    v_prompt = st("v_p").transpose(1, 0, 2, 3).reshape(DEPTH, 8, 16, 128, NH, DH)
    logf_prompt = st("lf_p").transpose(1, 0, 2, 3).reshape(DEPTH, 8, 16, 128, NH)
    conv_a_prompt = st("ca_p").transpose(1, 0, 2, 3)
    conv_b_prompt = st("cb_p").transpose(1, 0, 2, 3)
    k_sample = st("k_s").transpose(1, 0, 2, 3).reshape(DEPTH, 128, DSEQ, NH, DH)
    v_sample = st("v_s").transpose(1, 0, 2, 3).reshape(DEPTH, 128, DSEQ, NH, DH)
    logf_sample = st("lf_s").transpose(1, 0, 2, 3).reshape(DEPTH, 128, DSEQ, NH)
    conv_a_sample = st("ca_s").transpose(1, 0, 2, 3).reshape(DEPTH, 128, 2, 256)
    conv_b_sample = st("cb_s").transpose(1, 0, 2, 3, 4).reshape(DEPTH, 128, 30, 256)
    outs = (y_prompt, y_sample, k_prompt, v_prompt, logf_prompt, conv_a_prompt, conv_b_prompt,
            k_sample, v_sample, logf_sample, conv_a_sample, conv_b_sample)
    return tuple(np.ascontiguousarray(o, dtype=np.float32) for o in outs)
```
